# Optimizing a Trainium2 kernel written in Bass

```python
import math
import jax, jax.numpy as jnp
from jax import lax
import numpy as np

D_MODEL = 1024
BATCH = 2
SEQ = 8192
DEPTH = 1

POOL_WINDOWS = (2, 4, 8, 16)
N_POOL_GROUPS = 4
POOL_WIDTH = D_MODEL // 2
POOL_GROUP = POOL_WIDTH // N_POOL_GROUPS
POOL_OUT_GROUP = D_MODEL // N_POOL_GROUPS
DN_HEADS = 8
DN_HEAD_DIM = 128
DN_WIDTH = DN_HEADS * DN_HEAD_DIM
CONV_K = 4
CHUNK = 64
D_FF = 4 * D_MODEL
PLE_DIM = 256
LN_EPS = 1e-5
RMS_EPS = 1e-6
L2_EPS = 1e-6
DEEPNORM_ALPHA = (2.0 * DEPTH) ** 0.25
DEEPNORM_BETA = (8.0 * DEPTH) ** -0.25
QKV_WIDTH = 3 * DN_WIDTH
IN_WIDTH = POOL_WIDTH + QKV_WIDTH + DN_WIDTH + DN_HEADS + DN_HEADS + 2 * D_MODEL
SPLIT_POINTS = (
    POOL_WIDTH,
    POOL_WIDTH + QKV_WIDTH,
    POOL_WIDTH + QKV_WIDTH + DN_WIDTH,
    POOL_WIDTH + QKV_WIDTH + DN_WIDTH + DN_HEADS,
    POOL_WIDTH + QKV_WIDTH + DN_WIDTH + 2 * DN_HEADS,
    POOL_WIDTH + QKV_WIDTH + DN_WIDTH + 2 * DN_HEADS + D_MODEL,
)

kernel_name = "hybrid_pool_gdn_deepnorm_ple"


def _layer_norm(x, g, b):
    xf = x.astype(jnp.float32)
    mu = jnp.mean(xf, axis=-1, keepdims=True)
    var = jnp.mean(jnp.square(xf - mu), axis=-1, keepdims=True)
    y = (xf - mu) * lax.rsqrt(var + LN_EPS) * g.astype(jnp.float32) + b.astype(jnp.float32)
    return y.astype(x.dtype)


def _multiscale_pool(u, pool_w, pool_scale):
    bsz, seq, _ = u.shape
    ug = u.astype(jnp.float32).reshape(bsz, seq, N_POOL_GROUPS, POOL_GROUP)
    cs = jnp.cumsum(ug, axis=1)
    outs = []
    for gi, w in enumerate(POOL_WINDOWS):
        c = cs[:, :, gi]
        prev = jnp.pad(c, ((0, 0), (w, 0), (0, 0)))[:, :seq]
        cnt = jnp.minimum(jnp.arange(1, seq + 1), w).astype(jnp.float32)[None, :, None]
        outs.append((c - prev) / cnt - ug[:, :, gi])
    d = jnp.stack(outs, axis=2).astype(u.dtype)
    y = jnp.einsum('bsgc,gcd->bsgd', d, pool_w).reshape(bsz, seq, D_MODEL)
    return y * pool_scale


def _causal_depthwise_conv_silu(x, w):
    ch = x.shape[-1]
    y = lax.conv_general_dilated(
        x, w[:, None, :].astype(x.dtype), window_strides=(1,),
        padding=((CONV_K - 1, 0),), dimension_numbers=('NWC', 'WIO', 'NWC'),
        feature_group_count=ch)
    return jax.nn.silu(y)


def _l2norm(t):
    return t * lax.rsqrt(jnp.sum(jnp.square(t), axis=-1, keepdims=True) + L2_EPS)


def _chunk_gated_delta_rule(q, k, v, beta, g):
    bsz, seq, nh, dk = q.shape
    dv = v.shape[-1]
    n_chunks = seq // CHUNK

    def to_chunks(t):
        t = t.reshape((bsz, n_chunks, CHUNK, nh) + t.shape[3:])
        return jnp.moveaxis(t, 3, 2)

    q, k, v, beta, g = (to_chunks(t) for t in (q, k, v, beta, g))
    gc = jnp.cumsum(g, axis=-1)
    idx = jnp.arange(CHUNK)
    incl = idx[:, None] >= idx[None, :]
    strict = idx[:, None] > idx[None, :]
    decay = jnp.exp(jnp.where(incl, gc[..., :, None] - gc[..., None, :], -jnp.inf))
    kb = k * beta[..., None]
    m = jnp.where(strict, jnp.einsum('bnhid,bnhjd->bnhij', kb, k) * decay, 0.0)
    a = m + jnp.eye(CHUNK, dtype=m.dtype)
    rhs = jnp.concatenate([v * beta[..., None], kb * jnp.exp(gc)[..., None]], axis=-1)
    sol = lax.linalg.triangular_solve(a, rhs, left_side=True, lower=True, unit_diagonal=True)
    u, w = sol[..., :dv], sol[..., dv:]
    attn = jnp.einsum('bnhid,bnhjd->bnhij', q, k) * decay
    qg = q * jnp.exp(gc)[..., None]
    gl = gc[..., -1]
    kg = k * jnp.exp(gl[..., None] - gc)[..., None]
    xs = tuple(jnp.moveaxis(t, 1, 0) for t in (qg, kg, u, w, attn, gl))

    def step(state, inp):
        qg_n, kg_n, u_n, w_n, attn_n, gl_n = inp
        v_new = u_n - jnp.einsum('bhcd,bhde->bhce', w_n, state)
        o = jnp.einsum('bhcd,bhde->bhce', qg_n, state) + jnp.einsum('bhij,bhje->bhie', attn_n, v_new)
        state = state * jnp.exp(gl_n)[..., None, None] + jnp.einsum('bhcd,bhce->bhde', kg_n, v_new)
        return state, o

    s0 = jnp.zeros((bsz, nh, dk, dv), jnp.float32)
    _, o = lax.scan(step, s0, xs)
    o = jnp.moveaxis(o, 0, 1)
    return jnp.moveaxis(o, 2, 3).reshape(bsz, seq, nh, dv)


def _gated_deltanet(qkv, z, beta_raw, a_raw, conv_w, a_log, dt_bias, o_norm_w):
    bsz, seq, _ = qkv.shape
    qkv = _causal_depthwise_conv_silu(qkv, conv_w)
    q, k, v = jnp.split(qkv.astype(jnp.float32), 3, axis=-1)
    shp = (bsz, seq, DN_HEADS, DN_HEAD_DIM)
    q = _l2norm(q.reshape(shp)) * (DN_HEAD_DIM ** -0.5)
    k = _l2norm(k.reshape(shp))
    v = v.reshape(shp)
    beta = jax.nn.sigmoid(beta_raw.astype(jnp.float32))
    g = -jnp.exp(a_log.astype(jnp.float32)) * jax.nn.softplus(
        a_raw.astype(jnp.float32) + dt_bias.astype(jnp.float32))
    o = _chunk_gated_delta_rule(q, k, v, beta, g)
    o = o * lax.rsqrt(jnp.mean(jnp.square(o), axis=-1, keepdims=True) + RMS_EPS)
    o = o * o_norm_w.astype(jnp.float32) * jax.nn.silu(z.astype(jnp.float32).reshape(shp))
    return o.reshape(bsz, seq, DN_WIDTH).astype(qkv.dtype)


def setup_inputs(seed: int = 0) -> dict:
    key = jax.random.key(seed)
    ks = jax.random.split(key, 24)
    f32 = jnp.float32
    nrm = lambda k, shape, s: jax.random.normal(k, shape, f32) * s
    x = jax.random.normal(ks[0], (BATCH, SEQ, D_MODEL), f32)
    p = jax.random.normal(ks[1], (DEPTH, BATCH, SEQ, PLE_DIM), f32)
    ln_in_g = 1.0 + nrm(ks[2], (D_MODEL,), 0.02)
    ln_in_b = nrm(ks[3], (D_MODEL,), 0.02)
    s_in = D_MODEL ** -0.5
    w_pool_qk = nrm(ks[4], (DEPTH, D_MODEL, POOL_WIDTH + 2 * DN_WIDTH), s_in)
    w_v = nrm(ks[5], (DEPTH, D_MODEL, DN_WIDTH), s_in * DEEPNORM_BETA)
    w_rest = nrm(ks[6], (DEPTH, D_MODEL, IN_WIDTH - POOL_WIDTH - QKV_WIDTH), s_in)
    w_in = jnp.concatenate([w_pool_qk, w_v, w_rest], axis=-1)
    pool_w = nrm(ks[7], (DEPTH, N_POOL_GROUPS, POOL_GROUP, POOL_OUT_GROUP), POOL_GROUP ** -0.5)
    pool_scale = 1.0 + nrm(ks[8], (DEPTH, D_MODEL), 0.02)
    conv_w = nrm(ks[9], (DEPTH, CONV_K, QKV_WIDTH), CONV_K ** -0.5)
    a_log = jnp.log(jax.random.uniform(ks[10], (DEPTH, DN_HEADS), f32, 1.0, 16.0))
    dt = jnp.exp(jax.random.uniform(ks[11], (DEPTH, DN_HEADS), f32, math.log(1e-3), math.log(1e-1)))
    dt_bias = dt + jnp.log(-jnp.expm1(-dt))
    o_norm_w = 1.0 + nrm(ks[12], (DEPTH, DN_HEAD_DIM), 0.02)
    w_out = nrm(ks[13], (DEPTH, D_MODEL, D_MODEL), s_in * DEEPNORM_BETA)
    ln1_g = 1.0 + nrm(ks[14], (DEPTH, D_MODEL), 0.02)
    ln1_b = nrm(ks[15], (DEPTH, D_MODEL), 0.02)
    w_up = nrm(ks[16], (DEPTH, D_MODEL, D_FF), s_in)
    w_down = nrm(ks[17], (DEPTH, D_FF, D_MODEL), D_FF ** -0.5 * DEEPNORM_BETA)
    ple_gate_w = nrm(ks[18], (DEPTH, D_MODEL, D_MODEL), s_in)
    ple_proj_w = nrm(ks[19], (DEPTH, PLE_DIM, D_MODEL), PLE_DIM ** -0.5 * DEEPNORM_BETA)
    ln2_g = 1.0 + nrm(ks[20], (DEPTH, D_MODEL), 0.02)
    ln2_b = nrm(ks[21], (DEPTH, D_MODEL), 0.02)
    return {"x": x, "p": p, "ln_in_g": ln_in_g, "ln_in_b": ln_in_b, "w_in": w_in,
            "pool_w": pool_w, "pool_scale": pool_scale, "conv_w": conv_w, "a_log": a_log,
            "dt_bias": dt_bias, "o_norm_w": o_norm_w, "w_out": w_out, "ln1_g": ln1_g,
            "ln1_b": ln1_b, "w_up": w_up, "w_down": w_down, "ple_gate_w": ple_gate_w,
            "ple_proj_w": ple_proj_w, "ln2_g": ln2_g, "ln2_b": ln2_b}


def reference(x, p, ln_in_g, ln_in_b, w_in, pool_w, pool_scale, conv_w, a_log, dt_bias,
              o_norm_w, w_out, ln1_g, ln1_b, w_up, w_down, ple_gate_w, ple_proj_w, ln2_g, ln2_b):
    h = _layer_norm(x, ln_in_g, ln_in_b)
    for i in range(DEPTH):
        proj = h @ w_in[i]
        pool_in, qkv, z, beta_raw, a_raw, gate_a, gate_b = jnp.split(proj, SPLIT_POINTS, axis=-1)
        y_a = _multiscale_pool(pool_in, pool_w[i], pool_scale[i])
        y_b = _gated_deltanet(qkv, z, beta_raw, a_raw, conv_w[i], a_log[i], dt_bias[i], o_norm_w[i])
        mixed = jax.nn.sigmoid(gate_a) * y_a + jax.nn.sigmoid(gate_b) * y_b
        h = _layer_norm(DEEPNORM_ALPHA * h + mixed @ w_out[i], ln1_g[i], ln1_b[i])
        mlp = jnp.square(jax.nn.relu(h @ w_up[i])) @ w_down[i]
        r = DEEPNORM_ALPHA * h + mlp
        ple = jax.nn.sigmoid(r @ ple_gate_w[i]) * (p[i] @ ple_proj_w[i])
        h = _layer_norm(r + ple, ln2_g[i], ln2_b[i])
    return h
```

```python
import numpy as np
import concourse.bass as bass
import concourse.mybir as mybir
from concourse.bass_utils import run_bass_kernel_spmd

F32 = mybir.dt.float32
BF16 = mybir.dt.bfloat16
ALU = mybir.AluOpType
AF = mybir.ActivationFunctionType

D = 1024
NH = 8
DFF = 4096
PLE = 256
ALPHA = 2.0 ** 0.25
LN_EPS = 1e-5
RMS_EPS = 1e-6
L2_EPS = 1e-6
ENGS = ("pe", "act", "dve", "pool", "sp")
SEM_LIMIT = 12000


class Buf:
    __slots__ = ("name", "w", "rs", "tr", "te", "tl")

    def __init__(self, name):
        self.name = name
        self.w = None
        self.rs = []
        self.tr = 0.0
        self.te = None
        self.tl = 0.0


class V:
    __slots__ = ("b", "ap")

    def __init__(self, b, ap):
        self.b = b
        self.ap = ap


class TB:
    def __init__(self, h, name):
        self.h = h
        self.b = Buf(name)

    def __getitem__(self, idx):
        return V(self.b, self.h[idx])


class Sched:
    def __init__(self, nc):
        self.nc = nc
        self.ops = {e: [] for e in ENGS}
        self.cnt = {e: 0 for e in ENGS}
        self.epoch = {e: 0 for e in ENGS}
        self.sems = {}
        self.known = {e: {} for e in ENGS}
        self.dma_cnt = {}
        self.n_sems = 0
        self.capture = None
        self.atomic = False
        self.tm = {e: 0.0 for e in ENGS}

    def _sem(self, key):
        if key not in self.sems:
            self.sems[key] = self.nc.alloc_semaphore("s_" + str(key).replace(":", "_"))
            self.n_sems += 1
        return self.sems[key]

    def _collect(self, eng, reads, writes):
        need = {}

        def add(tok):
            if tok is None:
                return
            if need.get(tok[0], 0) < tok[1]:
                need[tok[0]] = tok[1]
        for b in reads:
            add(b.w)
        for b in writes:
            add(b.w)
            for r in b.rs:
                add(r)
        kn = self.known[eng]
        out = []
        for k, v in need.items():
            if kn.get(k, 0) >= v:
                continue
            if eng == "pe" and k[0] == "pe" and isinstance(k, tuple):
                continue
            kn[k] = v
            out.append((k, v))
        return out

    @staticmethod
    def _mark(tok, reads, writes):
        for b in reads:
            b.rs.append(tok)
        for b in writes:
            b.w = tok
            b.rs = []

    XLAT = 0.7

    def _est(self, eng, reads, writes):
        t = self.tm[eng]
        for b in reads:
            t = max(t, b.tr + (self.XLAT if b.te != eng else 0.05))
        for b in writes:
            t = max(t, b.tr + (self.XLAT if b.te != eng else 0.05), b.tl + self.XLAT)
        return t

    def _commit(self, eng, reads, writes, cost, lat=0.0):
        st = self._est(eng, reads, writes)
        en = st + cost
        self.tm[eng] = en
        for b in reads:
            if b.tl < en + lat:
                b.tl = en + lat
        for b in writes:
            b.tr = en + lat
            b.te = eng
            b.tl = 0.0

    def begin_atomic(self):
        self.atomic = True

    def end_atomic(self):
        self.atomic = False
        if self.capture:
            t = self.capture[-1]
            self.capture[-1] = t[:6] + (False,) + t[7:]

    def op(self, eng, fn, reads=(), writes=(), hold=False, cost=None):
        if cost is None:
            cost = {"pe": 0.2, "act": 0.35, "dve": 0.3, "pool": 0.8, "sp": 0.05}[eng]
        if self.capture is not None:
            self.capture.append((0, eng, fn, list(reads), list(writes), None, hold or self.atomic, cost))
            return None
        self._commit(eng, reads, writes, cost)
        waits = self._collect(eng, reads, writes)
        if self.cnt[eng] >= SEM_LIMIT:
            self.epoch[eng] += 1
            self.cnt[eng] = 0
        self.cnt[eng] += 1
        key = (eng, self.epoch[eng])
        self._sem(key)
        tok = (key, self.cnt[eng])
        self.ops[eng].append((waits, fn, (key, 1)))
        self._mark(tok, reads, writes)
        return tok

    def dma(self, eng, fn, reads=(), writes=(), key=None):
        if self.capture is not None:
            self.capture.append((1, eng, fn, list(reads), list(writes), key, False, 0.05))
            return None
        self._commit(eng, reads, writes, 0.05, lat=2.0)
        waits = self._collect(eng, reads, writes)
        if key is None:
            key = "d:" + (writes[0].name if writes else reads[0].name)
        self._sem(key)
        self.dma_cnt[key] = self.dma_cnt.get(key, 0) + 16
        tok = (key, self.dma_cnt[key])
        self.ops[eng].append((waits, fn, (key, 16)))
        self._mark(tok, reads, writes)
        return tok

    def replay(self, lists, chunk=1):
        idx = [0] * len(lists)
        lists = [l for l in lists if l]
        idx = [0] * len(lists)
        while True:
            best, bt = None, None
            for li, l in enumerate(lists):
                if idx[li] < len(l):
                    k, eng, fn, r, w, key, hold, cost = l[idx[li]]
                    t = (self._est(eng, r, w), idx[li] / len(l))
                    if bt is None or t < bt:
                        best, bt = li, t
            if best is None:
                break
            l = lists[best]
            n_em = 0
            while idx[best] < len(l):
                k, eng, fn, r, w, key, hold, cost = l[idx[best]]
                idx[best] += 1
                n_em += 1
                if k == 0:
                    self.op(eng, fn, r, w, cost=cost)
                else:
                    self.dma(eng, fn, r, w, key)
                if not hold and n_em >= chunk:
                    break

    def wait_all(self, eng, toks):
        need = {}
        for t in toks:
            if t is not None and need.get(t[0], 0) < t[1]:
                need[t[0]] = t[1]
        self.ops[eng].append((list(need.items()), None, None))

    def barrier(self):
        toks = []
        for e in ENGS:
            if self.cnt[e] > 0:
                toks.append(((e, self.epoch[e]), self.cnt[e]))
        for k, v in self.dma_cnt.items():
            toks.append((k, v))
        for e in ENGS:
            self.wait_all(e, toks)
            for k, v in toks:
                if self.known[e].get(k, 0) < v:
                    self.known[e][k] = v

    def emit(self):
        nc = self.nc
        handles = {"pe": "tensor", "act": "scalar", "dve": "vector",
                   "pool": "gpsimd", "sp": "sync"}
        with nc.Block() as block:
            for e in ENGS:
                ops = self.ops[e]

                def body(engh, ops=ops):
                    for waits, fn, inc in ops:
                        for k, v in waits:
                            engh.wait_ge(self.sems[k], v)
                        if fn is None:
                            continue
                        ins = fn(engh)
                        ins.then_inc(self.sems[inc[0]], inc[1])
                getattr(block, handles[e])(body)


def build(NPRE, NOWN, debug=False, LVL=9, SUB=9, SIGM_OLD=False, RSQ_OLD=False, FL=("ln", "psilu", "osilu", "orsq")):
    NT = NPRE + NOWN
    NG = NOWN // 4
    nc = bass.Bass("TRN2", target_bir_lowering=False)
    S = Sched(nc)

    def din(name, shape, dt=F32):
        return nc.dram_tensor(name, list(shape), dt, kind="ExternalInput").ap()

    xp = din("xp", [NT * 128, D])
    p_own = din("p_own", [NOWN * 128, PLE])
    consts_d = din("consts", [128, 10 * 128])
    smalls_d = din("smalls", [128, 8 + 8 + NT + 16 + 16 + 64 + 2 + 96])
    tabs_d = din("tabs", [128, 7 * D + 128])
    w_kv_d = din("w_kv", [D, 2048])
    w_q_d = din("w_q", [D, 1024])
    w_zg_d = din("w_zg", [D, 2048])
    w_ga_d = din("w_ga", [D, 1024])
    w_ba_d = din("w_ba", [D, 16])
    w_pl_d = din("w_pl", [D, 512])
    poolw_d = din("poolw", [128, 4 * 256])
    w_out_d = din("w_out", [D, D])
    w_up_d = din("w_up", [D, DFF])
    w_down_d = din("w_down", [DFF, D])
    w_g_d = din("w_g", [D, D])
    w_p_d = din("w_p", [PLE, D])
    out_d = nc.dram_tensor("out", [NOWN * 128, D], F32, kind="ExternalOutput").ap()
    acc0_d = nc.dram_tensor("acc0_scr", [NOWN * 128, D], F32, kind="Internal").ap()
    h1T_d = nc.dram_tensor("h1T_scr", [128, 8, NOWN * 128], BF16, kind="Internal").ap()
    dbg_d = None
    if debug:
        dbg_d = nc.dram_tensor("dbg", [NOWN * 128, D], F32, kind="ExternalOutput").ap()

    used = [0]

    def sb(name, shape, dt=F32):
        n = 1
        for s in shape[1:]:
            n *= s
        used[0] += n * (4 if dt == F32 else 2)
        return TB(nc.alloc_sbuf_tensor("sb_" + name, list(shape), dt), name)

    def bufs(vs):
        return [v.b for v in vs if isinstance(v, V)]

    def apof(x):
        return x.ap if isinstance(x, V) else x

    def fsz(ap):
        n = 1
        for d_ in ap.shape[1:]:
            n *= d_
        return n

    def MM(o, l, r, start=True, stop=True, hold=None):
        c = 0.12 + fsz(r.ap) * (4 if l.ap.dtype == F32 else 1) / 2400.0
        S.op("pe", lambda e: e.matmul(o.ap, lhsT=l.ap, rhs=r.ap, start=start, stop=stop),
             reads=[l.b, r.b], writes=[o.b], hold=((not stop) if hold is None else hold), cost=c)

    def TR(o, i):
        S.op("pe", lambda e: e.transpose(o.ap, i.ap, identb[:, :].ap),
             reads=[i.b, identb.b], writes=[o.b])

    def ACT(o, i, func, bias=None, scale=None, accum=None):
        kw = {}
        if bias is not None:
            kw["bias"] = apof(bias)
        if scale is not None:
            kw["scale"] = apof(scale)
        if accum is not None:
            kw["accum_out"] = accum.ap
        w = [o.b] + ([accum.b] if accum is not None else [])
        S.op("act", lambda e: e.activation(out=o.ap, in_=i.ap, func=func, **kw),
             reads=bufs([i, bias, scale]), writes=w, cost=0.22 + fsz(o.ap) / 1200.0)

    def TS(eng, o, i, s1, s2, op0, op1=None):
        if op1 is None:
            S.op(eng, lambda e: e.tensor_scalar(out=o.ap, in0=i.ap, scalar1=apof(s1), scalar2=None, op0=op0),
                 reads=bufs([i, s1]), writes=[o.b], cost=(0.08 + fsz(o.ap) / 960.0) if eng == "dve" else (0.15 + fsz(o.ap) / 400.0))
        else:
            S.op(eng, lambda e: e.tensor_scalar(out=o.ap, in0=i.ap, scalar1=apof(s1), scalar2=apof(s2),
                                                op0=op0, op1=op1),
                 reads=bufs([i, s1, s2]), writes=[o.b], cost=(0.08 + fsz(o.ap) / 960.0) if eng == "dve" else (0.15 + fsz(o.ap) / 400.0))

    def TT(eng, o, a, b, op):
        S.op(eng, lambda e: e.tensor_tensor(out=o.ap, in0=a.ap, in1=b.ap, op=op),
             reads=[a.b, b.b], writes=[o.b], cost=(0.08 + fsz(o.ap) / 960.0) if eng == "dve" else (0.15 + fsz(o.ap) / 400.0))

    def STT(eng, o, a, s, b, op0, op1):
        S.op(eng, lambda e: e.scalar_tensor_tensor(out=o.ap, in0=a.ap, scalar=apof(s), in1=b.ap, op0=op0, op1=op1),
             reads=bufs([a, s, b]), writes=[o.b], cost=(0.08 + fsz(o.ap) / 960.0) if eng == "dve" else (0.15 + fsz(o.ap) / 400.0))

    def CP(eng, o, i):
        if eng == "act":
            S.op("act", lambda e: e.copy(out=o.ap, in_=i.ap), reads=[i.b], writes=[o.b],
                 cost=0.22 + fsz(o.ap) / 1200.0)
        else:
            S.op(eng, lambda e: e.tensor_copy(out=o.ap, in_=i.ap), reads=[i.b], writes=[o.b], cost=(0.08 + fsz(o.ap) / 960.0) if eng == "dve" else (0.15 + fsz(o.ap) / 400.0))

    def MSET(eng, o, val):
        S.op(eng, lambda e: e.memset(o.ap, val), reads=[], writes=[o.b])

    def DMA(eng, o, i, key=None):
        return S.dma(eng, lambda e: e.dma_start(out=apof(o), in_=apof(i)),
                     reads=bufs([i]), writes=bufs([o]), key=key)

    def SIGM(o, i):
        if SIGM_OLD:
            ACT(o, i, AF.Sigmoid)
            return
        ACT(o, i, AF.Exp, scale=-1.0)
        ACT(o, o, AF.Ln, bias=1.0)
        ACT(o, o, AF.Exp, scale=-1.0)

    def RSQ(o, i, scale, bias):
        if RSQ_OLD:
            ACT(o, i, AF.Sqrt, bias=bias, scale=scale)
            RECIP(o, o)
            return
        ACT(o, i, AF.Ln, bias=bias, scale=scale)
        ACT(o, o, AF.Exp, scale=-0.5)

    def RECIP(o, i):
        S.op("dve", lambda e: e.reciprocal(out=o.ap, in_=i.ap), reads=[i.b], writes=[o.b],
             cost=0.1 + fsz(o.ap) / 400.0)

    class Slot:
        def __init__(self, h, c0, w, b):
            self.h, self.c0, self.w, self.b = h, c0, w, b

        def v(self, c0=0, c1=None, r0=0, r1=128):
            c1 = self.w if c1 is None else c1
            return V(self.b, self.h[r0:r1, self.c0 + c0:self.c0 + c1])

    class RR:
        def __init__(self, banks):
            self.banks, self.i = banks, 0

        def get(self):
            return self.group(1)[0]

        def group(self, n):
            bk = self.banks[self.i % len(self.banks)]
            self.i += 1
            return bk[:n]

        def get_bank(self):
            return self.group(len(self.banks[0]))

    def mkbank(name, dt, ncols, w):
        hq = nc.alloc_psum_tensor(name, [128, ncols], dt)
        b = Buf(name)
        return [Slot(hq, i * w, w, b) for i in range(ncols // w)]
    ptbanks = [mkbank("ptb%d" % i, BF16, 1024, 128) for i in range(2)]
    PT = RR(ptbanks)
    PTL = [RR([ptbanks[0]]), RR([ptbanks[1]])]
    fbanks = [mkbank("pf%d" % i, F32, 512, 128) for i in range(4)]
    wbanks = [[Slot(bk[0].h, 0, 512, bk[0].b)] for bk in fbanks]
    PQ = RR([fbanks[0], fbanks[2], fbanks[1], fbanks[3]])
    PQL = [RR([fbanks[0], fbanks[1]]), RR([fbanks[2], fbanks[3]])]
    pobanks = [mkbank("po%d" % i, F32, 512, 128) for i in range(2)]
    POL = [RR([pobanks[0]]), RR([pobanks[1]])]
    PW = RR([wbanks[0], wbanks[2], wbanks[1], wbanks[3]])

    consts = sb("consts", [128, 10 * 128])
    (C_ID, C_UBLK, C_BLK, C_MA, C_MB, C_UALL, C_BST, C_MLS, C_MUI, C_ONE) = range(10)

    def cst(i):
        return consts[:, i * 128:(i + 1) * 128]
    identb = sb("identb", [128, 128], BF16)
    onesb = sb("onesb", [128, 128], BF16)
    NSM = 8 + 8 + NT + 16 + 16 + 64 + 2 + 96
    smalls = sb("smalls", [128, NSM])
    o = 0
    lng_c = smalls[:, o:o + 8]; o += 8
    lnb_c = smalls[:, o:o + 8]; o += 8
    o_mask = o; o += NT
    alog_t = smalls[:, o:o + 8]; dtb_t = smalls[:, o + 8:o + 16]; o += 16
    zcol = smalls[:, o:o + 1]
    o += 16
    o_invc = o; o += 64
    pmA = smalls[:, o:o + 1]; pmB = smalls[:, o + 1:o + 2]; o += 2
    o_convw = o; o += 96
    (T_G0, T_B0, T_PS, T_G1, T_B1, T_G2, T_B2) = range(7)

    def tabd(i):
        return tabs_d[:, i * D:(i + 1) * D]
    onw_t = sb("onw", [128, 128])
    negA = sb("negA", [128, 8])
    tmpw = sb("tmpw", [128, D])
    ybn = sb("ybn", [128, D])
    yas = sb("yas", [128, D])

    WOFF = {}
    woff = 0
    for nm, ncols in (("kv", 2048), ("q", 1024), ("zg", 2048), ("ga", 1024), ("ba", 16),
                      ("pl", 512), ("out", 1024)):
        WOFF[nm] = (woff, ncols)
        woff += 8 * ncols
    WOFF["poolw"] = (woff, 1024)
    woff += 1024
    WBIG = sb("wbig", [128, woff], BF16)

    WB = {nm: Buf("w_" + nm) for nm in WOFF}

    def wv(nm, kc, c0, c1):
        off, ncols = WOFF[nm]
        return V(WB[nm], WBIG.h[:, off + kc * ncols + c0: off + kc * ncols + c1])

    def wload(nm, src):
        off, ncols = WOFF[nm]
        dst = WBIG.h[:, off:off + 8 * ncols].rearrange("p (k c) -> p k c", k=8)
        DMA("pool", V(WB[nm], dst), src.rearrange("(k p) c -> p k c", p=128))

    def wview(off, shape, dt, name):
        n = 1
        for s in shape[1:]:
            n *= s
        ap = WBIG.h[:, off:off + n * (2 if dt == F32 else 1)]
        if dt == F32:
            ap = ap.bitcast(F32)
        if len(shape) == 3:
            ap = ap.rearrange("p (k c) -> p k c", k=shape[1])
        t = TB.__new__(TB)
        t.h = ap
        t.b = Buf(name)
        return t

    DMA("sp", consts[:, :], consts_d)
    DMA("sp", smalls[:, :], smalls_d)
    DMA("sp", onw_t[:, :], tabs_d[:, 7 * D:7 * D + 128])
    DMA("pool", identb[:, :], consts_d[:, 0:128])
    DMA("pool", onesb[:, :], consts_d[:, C_ONE * 128:(C_ONE + 1) * 128])
    wload("kv", w_kv_d)
    wload("ba", w_ba_d)
    off_pw, _ = WOFF["poolw"]

    def load_own_weights():
        wload("q", w_q_d)
        wload("zg", w_zg_d)
        wload("ga", w_ga_d)
        wload("pl", w_pl_d)
        wload("out", w_out_d)
        DMA("sp", yas[:, :], poolw_d)
        DMA("sp", tmpw[:, :], tabd(T_PS))
        TT("pool", yas[:, :], yas[:, :], tmpw[:, :], ALU.mult)
        CP("pool", V(WB["poolw"], WBIG.h[:, off_pw:off_pw + 1024]), yas[:, :])
    ACT(negA[:, :], alog_t, AF.Exp)
    TS("dve", negA[:, :], negA[:, :], -1.0, None, ALU.mult)

    xt = sb("xt", [128, D])
    xn = sb("xn", [128, D], BF16)
    hT = sb("hT", [128, 8, 128], BF16)
    bst = sb("bst", [128, 2, 6])
    mv = sb("mv", [128, 8])
    gm = sb("gm", [128, 16])
    NCH = 24
    pre = [sb("pre%d" % c, [128, 131]) for c in range(NCH)]
    for c in range(NCH):
        MSET("pool", pre[c][:, 0:3], 0.0)
    cacc = [sb("cacc%d" % i, [128, 128]) for i in range(2)]
    ksil = [sb("ksil%d" % i, [128, 128]) for i in range(2)]
    ksq = [sb("ksq%d" % i, [128, 128], BF16) for i in range(2)]
    rnb = [sb("rnb%d" % i, [128, 128]) for i in range(2)]
    khatT = [sb("khatT%d" % h, [128, 128], BF16) for h in range(2)]
    qhatT = [sb("qhatT%d" % h, [128, 128], BF16) for h in range(2)]
    vTb = [sb("vT%d" % i, [128, 128], BF16) for i in range(2)]
    bv = [sb("bv%d" % h, [128, 128], BF16) for h in range(2)]
    kbg = [sb("kbg%d" % h, [128, 128], BF16) for h in range(2)]
    kgA = [sb("kgA%d" % h, [128, 128], BF16) for h in range(2)]
    kgB = [sb("kgB%d" % h, [128, 128], BF16) for h in range(2)]
    sm = sb("sm", [128, 160])
    S32 = [sb("S32_%d" % h, [128, 128]) for h in range(NH)]
    Sbf = [sb("Sbf_%d" % h, [128, 128], BF16) for h in range(NH)]
    for h in range(NH):
        MSET("pool", S32[h][:, :], 0.0)
        MSET("pool", Sbf[h][:, :], 0.0)
    Amat = [sb("Amat%d" % i, [128, 128]) for i in range(2)]
    Grep = [sb("Grep%d" % i, [128, 128]) for i in range(2)]
    Dex = [sb("Dex%d" % i, [128, 128]) for i in range(2)]
    Dl = [sb("Dl%d" % i, [128, 128]) for i in range(2)]
    DTe = [sb("DTe%d" % i, [128, 128]) for i in range(2)]
    Ebc = [sb("Ebc%d" % i, [128, 128]) for i in range(2)]
    NPB = 12
    pb = [[sb("pb%d_%d" % (l, i), [128, 128], BF16) for i in range(NPB)] for l in range(2)]
    pbi = [0, 0]

    def newpb_l(l):
        t = pb[l][pbi[l] % NPB]
        pbi[l] += 1
        return t
    u32 = [sb("u32_%d" % i, [128, 128]) for i in range(2)]
    wTb = [sb("wTb%d" % i, [128, 128], BF16) for i in range(2)]
    vnew = [sb("vnew%d" % i, [128, 128], BF16) for i in range(2)]
    attnT = [sb("attnT%d" % i, [128, 128], BF16) for i in range(2)]
    qgTa = [sb("qgTa%d" % i, [128, 128], BF16) for i in range(2)]
    qgTb = [sb("qgTb%d" % i, [128, 128], BF16) for i in range(2)]
    for i in range(2):
        MSET("pool", vnew[i][:, :], 0.0)
        MSET("pool", qgTa[i][:, :], 0.0)
        MSET("pool", qgTb[i][:, :], 0.0)
    rs = sb("rs", [128, 16])
    junk = sb("junk", [128, 128])
    mixb = sb("mixb", [128, D], BF16)
    mixT = sb("mixT", [128, 8, 128], BF16)
    pbuf = [sb("pbuf%d" % g, [128, 143]) for g in range(4)]
    ptmp = [sb("ptmp%d" % i, [128, 143]) for i in range(2)]
    for g in range(4):
        MSET("pool", pbuf[g][:, 0:15], 0.0)
    dTp = [sb("dTp%d" % g, [128, 128], BF16) for g in range(4)]

    def lnorm(src, eps):
        for c in range(2):
            S.op("dve", lambda e, c=c: e.bn_stats(out=bst.h[:, c, :], in_=src.h[:, c * 512:(c + 1) * 512]),
                 reads=[src.b], writes=[bst.b])
        S.op("dve", lambda e: e.bn_aggr(out=mv.h[:, 0:2], in_=bst.h[:, :, :]), reads=[bst.b], writes=[mv.b])
        if "ln" in FL:
            RSQ(mv[:, 3:4], mv[:, 1:2], 1.0, eps)
        else:
            ACT(mv[:, 2:3], mv[:, 1:2], AF.Sqrt, bias=eps)
            RECIP(mv[:, 3:4], mv[:, 2:3])
        STT("dve", mv[:, 4:5], mv[:, 0:1], -1.0, mv[:, 3:4], ALU.mult, ALU.mult)
        return mv[:, 3:4], mv[:, 4:5]

    def affine(dst, ig, ib):
        DMA("sp", tmpw[:, :], tabd(ig))
        TT("pool", dst[:, :], dst[:, :], tmpw[:, :], ALU.mult)
        DMA("sp", tmpw[:, :], tabd(ib))
        TT("pool", dst[:, :], dst[:, :], tmpw[:, :], ALU.add)

    NSTP = NPRE // 4
    scrA = [WOFF["q"][0], WOFF["ba"][0]]
    scrB = [WOFF["pl"][0], WOFF["poolw"][0] + 1024]

    def salloc(shape, dt, name, reg=None):
        reg = scrA if reg is None else reg
        n = 1
        for s_ in shape[1:]:
            n *= s_
        ne = n * (2 if dt == F32 else 1)
        if reg[0] % 2:
            reg[0] += 1
        t = wview(reg[0], shape, dt, name)
        reg[0] += ne
        assert reg[0] <= reg[1], (name, reg)
        return t

    hT4 = [salloc([128, 8, 512], BF16, "hT4_%d" % i, scrB) for i in range(2)]
    smx2 = [salloc([128, 16 * 32], F32, "smx%d" % i) for i in range(2)]
    hist = salloc([128, 16, 3], F32, "hist")
    workb = [salloc([128, 515], F32, "work%d" % i) for i in range(2)]
    caccb = [salloc([128, 512], F32, "caccb%d" % i) for i in range(2)]
    rnb4 = salloc([128, 512], F32, "rnb4")
    ksq4 = salloc([128, 512], BF16, "ksq4")
    khT4 = [salloc([128, 512], BF16, "khT4_%d" % i) for i in range(2)]
    vT4 = salloc([128, 512], BF16, "vT4")
    bv4 = [salloc([128, 4, 128], BF16, "bv4_%d" % i) for i in range(2)]
    kbg4 = [salloc([128, 4, 128], BF16, "kbg4_%d" % i) for i in range(2)]
    kgA4 = [salloc([128, 4, 128], BF16, "kgA4_%d" % i) for i in range(3)]
    kgB4 = [salloc([128, 4, 128], BF16, "kgB4_%d" % i) for i in range(3)]
    Am4 = salloc([128, 4, 128], F32, "Am4")
    Dex4 = salloc([128, 4, 128], F32, "Dex4")
    NPC = 10
    pch = [salloc([128, 4, 128], BF16, "pch%d" % i) for i in range(NPC)]
    pci = [0]

    def newpc():
        t = pch[pci[0] % NPC]
        pci[0] += 1
        return t
    u4 = [salloc([128, 4, 128], F32, "u4_%d" % i) for i in range(2)]
    wT4 = [salloc([128, 4, 128], BF16, "wT4_%d" % i) for i in range(2)]
    vn4 = [salloc([128, 128], BF16, "vn4_%d" % i) for i in range(2)]
    if NSTP > 0:
        MSET("pool", hist[:, :, :], 0.0)
        for i in range(2):
            MSET("pool", vn4[i][:, :], 0.0)

    F1R = RR([fbanks[0], fbanks[1]])
    F2R = RR([fbanks[2], fbanks[3]])
    RC = RR(pobanks)
    PT1 = RR([ptbanks[0]])
    PT2 = RR([ptbanks[1]])

    def bank3(bk, n=4, w=128):
        return V(bk[0].b, bk[0].h[:, 0:n * w].rearrange("p (s c) -> p s c", s=n))

    def bankw(bk):
        return V(bk[0].b, bk[0].h[:, 0:512])

    def bc_last(v2, n):
        return V(v2.b, v2.ap.unsqueeze(2).to_broadcast([128, v2.ap.shape[1], n]))

    def bc_mid(v2, s_):
        return V(v2.b, v2.ap.unsqueeze(1).to_broadcast([128, s_, v2.ap.shape[1]]))

    def smq(T, q):
        return smx2[T % 2][:, q * 32:(q + 1) * 32]

    def smq3(T, q):
        sx = smx2[T % 2]
        return V(sx.b, sx.h[:, q * 32:(q + 1) * 32].rearrange("p (s h) -> p s h", h=8))

    def smh(T, q, h):
        sx = smx2[T % 2]
        return V(sx.b, sx.h[:, q * 32:(q + 1) * 32].rearrange("p (s h) -> p h s", h=8)[:, h, :])

    def smc(T, q, s_, h):
        return smx2[T % 2][:, q * 32 + s_ * 8 + h:q * 32 + s_ * 8 + h + 1]
    (Q_BETA, Q_NBETA, Q_G, Q_EGC, Q_BGE, Q_EKGA, Q_EKGB, Q_EGLA, Q_EGLB, Q_T1, Q_T2) = range(11)

    def pre_f1(g):
        T, h = divmod(g, 8)
        hTT = hT4[T % 2]
        for kind in ("k", "v"):
            c = (0 if kind == "k" else 8) + h
            ki = 0 if kind == "k" else 1
            pp = F1R.get_bank()
            for kc in range(8):
                MM(bankw(pp), wv("kv", kc, c * 128, c * 128 + 128), hTT[:, kc, :],
                   start=(kc == 0), stop=(kc == 7))
            wk = workb[ki]
            CP("pool", wk[:, 0:3], hist[:, c, :])
            CP("act", wk[:, 3:515], bankw(pp))
            ca = caccb[ki]
            cw = [smalls[:, o_convw + c * 4 + k:o_convw + c * 4 + k + 1] for k in range(4)]
            TS("dve", ca[:, :], wk[:, 3:515], cw[3], None, ALU.mult)
            STT("dve", ca[:, :], wk[:, 2:514], cw[2], ca[:, :], ALU.mult, ALU.add)
            STT("dve", ca[:, :], wk[:, 1:513], cw[1], ca[:, :], ALU.mult, ALU.add)
            STT("dve", ca[:, :], wk[:, 0:512], cw[0], ca[:, :], ALU.mult, ALU.add)
            CP("pool", hist[:, c, :], wk[:, 512:515])
            if "psilu" in FL:
                SIGM(rnb4[:, :], ca[:, :])
            else:
                ACT(rnb4[:, :], ca[:, :], AF.Exp, scale=-1.0)
                TS("pool", rnb4[:, :], rnb4[:, :], 1.0, None, ALU.add)
                RECIP(rnb4[:, :], rnb4[:, :])
            if kind == "v":
                TT("pool", vT4[:, :], ca[:, :], rnb4[:, :], ALU.mult)
                ptb_ = PT1.get_bank()
                for s_ in range(4):
                    TR(ptb_[s_].v(), vT4[:, s_ * 128:(s_ + 1) * 128])
                TT("dve", bv4[g % 2][:, :, :], bank3(ptb_), bc_last(smh(T, Q_BETA, h), 128), ALU.mult)
            else:
                TT("pool", ca[:, :], ca[:, :], rnb4[:, :], ALU.mult)
                ACT(ksq4[:, :], ca[:, :], AF.Square)
                pss = F1R.get_bank()
                MM(bankw(pss), onesb[:, :], ksq4[:, :])
                ACT(rnb4[:, :], bankw(pss), AF.Ln, bias=L2_EPS)
                ACT(rnb4[:, :], rnb4[:, :], AF.Exp, scale=-0.5)
                TT("pool", khT4[g % 2][:, :], ca[:, :], rnb4[:, :], ALU.mult)
                ptb_ = PT1.get_bank()
                for s_ in range(4):
                    TR(ptb_[s_].v(), khT4[g % 2][:, s_ * 128:(s_ + 1) * 128])
                TT("dve", kbg4[g % 2][:, :, :], bank3(ptb_), bc_last(smh(T, Q_BGE, h), 128), ALU.mult)
                TT("dve", kgA4[g % 3][:, :, :], bank3(ptb_), bc_last(smh(T, Q_EKGA, h), 128), ALU.mult)
                TT("dve", kgB4[g % 3][:, :, :], bank3(ptb_), bc_last(smh(T, Q_EKGB, h), 128), ALU.mult)

    def pre_f2(g):
        T, h = divmod(g, 8)
        st = g % 2
        for s_ in range(4):
            ACT(Am4[:, s_, :], cst(C_UALL), AF.Identity, bias=zcol, scale=smc(T, Q_G, s_, h))
        pd = F2R.get_bank()
        for s_ in range(4):
            MM(pd[s_].v(), Am4[:, s_, :], cst(C_BST))
        ACT(Dex4[:, :, :], bank3(pd), AF.Exp)
        TT("pool", Dex4[:, :, :], Dex4[:, :, :], bc_mid(cst(C_MLS), 4), ALU.mult)
        TT("pool", Dex4[:, :, :], Dex4[:, :, :], bc_last(smh(T, Q_NBETA, h), 128), ALU.mult)
        pg = F2R.get_bank()
        for s_ in range(4):
            ks_ = khT4[st][:, s_ * 128:(s_ + 1) * 128]
            MM(pg[s_].v(), ks_, ks_)
        P = newpc()
        TT("dve", P[:, :, :], bank3(pg), Dex4[:, :, :], ALU.mult)
        ptb_ = PT2.get_bank()
        for s_ in range(4):
            TR(ptb_[s_].v(), P[:, s_, :])
        PTt = newpc()
        CP("act", PTt[:, :, :], bank3(ptb_))
        TTk = newpc()
        TT("pool", TTk[:, :, :], PTt[:, :, :], bc_mid(identb[:, :], 4), ALU.add)
        for lev in range(5):
            pn = F2R.get_bank()
            for s_ in range(4):
                MM(pn[s_].v(), PTt[:, s_, :], P[:, s_, :])
            Pn = newpc()
            CP("act", Pn[:, :, :], bank3(pn))
            if lev < 4:
                pnt = F2R.get_bank()
                for s_ in range(4):
                    MM(pnt[s_].v(), P[:, s_, :], PTt[:, s_, :])
                PnT = newpc()
                CP("act", PnT[:, :, :], bank3(pnt))
            px = F2R.get_bank()
            for s_ in range(4):
                MM(px[s_].v(), Pn[:, s_, :], TTk[:, s_, :])
            TTn = newpc()
            TT("dve", TTn[:, :, :], bank3(px), TTk[:, :, :], ALU.add)
            P, TTk = Pn, TTn
            if lev < 4:
                PTt = PnT
        pu = F2R.get_bank()
        for s_ in range(4):
            MM(pu[s_].v(), TTk[:, s_, :], bv4[st][:, s_, :])
        CP("act", u4[st][:, :, :], bank3(pu))
        pwt = F2R.get_bank()
        for s_ in range(4):
            MM(pwt[s_].v(), kbg4[st][:, s_, :], TTk[:, s_, :])
        CP("act", wT4[st][:, :, :], bank3(pwt))

    def pre_recur(g):
        T, h = divmod(g, 8)
        st = g % 2
        vn = vn4[st]
        for s_ in range(4):
            for (lo, hi, kg, qe) in ((0, 64, kgA4, Q_EGLA), (64, 128, kgB4, Q_EGLB)):
                pws = RC.get_bank()
                MM(pws[0].v(), wT4[st][:, s_, :], Sbf[h][:, :])
                TT("dve", vn[lo:hi, :], u4[st][lo:hi, s_, :], pws[0].v(r0=lo, r1=hi), ALU.subtract)
                psu = RC.get_bank()
                MM(psu[0].v(), kg[g % 3][:, s_, :], vn[:, :])
                STT("dve", Sbf[h][:, :], S32[h][:, :], smc(T, qe, s_, h), psu[0].v(), ALU.mult, ALU.add)
                STT("dve", S32[h][:, :], S32[h][:, :], smc(T, qe, s_, h), psu[0].v(), ALU.mult, ALU.add)

    def pre_amble(T):
        hTT = hT4[T % 2]
        for s_ in range(4):
            n = 4 * T + s_
            DMA("sp", xt[:, :], xp[n * 128:(n + 1) * 128, :])
            rstd, nb = lnorm(xt, LN_EPS)
            ACT(xn[:, :], xt[:, :], AF.Identity, bias=nb, scale=rstd)
            mcol = smalls[:, o_mask + n:o_mask + n + 1]
            TS("dve", gm[:, 0:8], lng_c, mcol, None, ALU.mult)
            TS("dve", gm[:, 8:16], lnb_c, mcol, None, ALU.mult)
            S.begin_atomic()
            ptg = PT1.get_bank()
            for kc in range(8):
                TR(ptg[kc].v(), xn[:, kc * 128:(kc + 1) * 128])
            for kc in range(8):
                TS("dve", hTT[:, kc, s_ * 128:(s_ + 1) * 128], ptg[kc].v(), gm[:, kc:kc + 1],
                   gm[:, 8 + kc:9 + kc], ALU.mult, ALU.add)
            S.end_atomic()
        S.begin_atomic()
        pba = F1R.get_bank()
        for s_ in range(4):
            for kc in range(8):
                MM(V(pba[0].b, pba[0].h[:, s_ * 16:s_ * 16 + 16]), hTT[:, kc, s_ * 128:(s_ + 1) * 128],
                   wv("ba", kc, 0, 16), start=(kc == 0), stop=(kc == 7))
        pba3 = pba[0].h[:, 0:64].rearrange("p (s c) -> p s c", s=4)
        braw = V(pba[0].b, pba3[:, :, 0:8])
        araw = V(pba[0].b, pba3[:, :, 8:16])
        ACT(smq3(T, Q_T1), braw, AF.Exp, scale=-1.0)
        TT("dve", smq3(T, Q_T2), araw, bc_mid(dtb_t, 4), ALU.add)
        S.end_atomic()
        TS("dve", smq(T, Q_T1), smq(T, Q_T1), 1.0, None, ALU.add)
        RECIP(smq(T, Q_BETA), smq(T, Q_T1))
        TS("dve", smq(T, Q_NBETA), smq(T, Q_BETA), -1.0, None, ALU.mult)
        ACT(smq(T, Q_T2), smq(T, Q_T2), AF.Exp)
        ACT(smq(T, Q_T2), smq(T, Q_T2), AF.Ln, bias=1.0)
        TT("dve", smq3(T, Q_G), smq3(T, Q_T2), bc_mid(negA[:, :], 4), ALU.mult)
        S.begin_atomic()
        pgc = F1R.get_bank()
        for i, ci in enumerate((C_UBLK, C_BLK, C_MA, C_MB)):
            MM(V(pgc[0].b, pgc[0].h[:, i * 32:i * 32 + 32]), cst(ci), smq(T, Q_G))

        def pg_(i):
            return V(pgc[0].b, pgc[0].h[:, i * 32:i * 32 + 32])
        ACT(smq(T, Q_EGC), pg_(0), AF.Exp)
        CP("act", smq(T, Q_T1), pg_(1))
        ACT(smq(T, Q_EGLA), pg_(2), AF.Exp)
        ACT(smq(T, Q_EGLB), pg_(3), AF.Exp)
        TT("dve", smq(T, Q_T2), smq(T, Q_T1), pg_(0), ALU.subtract)
        S.end_atomic()
        TT("dve", smq(T, Q_BGE), smq(T, Q_EGC), smq(T, Q_BETA), ALU.mult)
        ACT(smq(T, Q_T2), smq(T, Q_T2), AF.Exp)
        TS("dve", smq(T, Q_EKGA), smq(T, Q_T2), pmA, None, ALU.mult)
        TS("dve", smq(T, Q_EKGB), smq(T, Q_T2), pmB, None, ALU.mult)

    def cap(fn, *args):
        S.capture = []
        fn(*args)
        l = S.capture
        S.capture = None
        return l

    GT = NSTP * 8
    if NSTP > 0:
        S.replay([cap(pre_amble, 0)])
    nextpre, ppos, pstep = [], 0, 0
    for g in range(GT + 2 if NSTP > 0 else 0):
        lists = []
        if 0 <= g - 2 < GT:
            lists.append(cap(pre_recur, g - 2))
        if 0 <= g - 1 < GT:
            lists.append(cap(pre_f2, g - 1))
        if g < GT:
            lists.append(cap(pre_f1, g))
            T, h = divmod(g, 8)
            if T + 1 < NSTP:
                if h == 2:
                    nextpre = cap(pre_amble, T + 1)
                    ppos = 0
                    pstep = (len(nextpre) + 4) // 5
                if 2 <= h < 7:
                    pend = min(ppos + pstep, len(nextpre))
                    while 0 < pend < len(nextpre) and nextpre[pend - 1][6]:
                        pend += 1
                    lists.append(nextpre[ppos:pend])
                    ppos = pend
        S.replay(lists, chunk=2)

    if NSTP > 0:
        for c in range(16):
            CP("pool", pre[c][:, 0:3], hist[:, c, :])
        CP("pool", hT[:, :, :], hT4[(NSTP - 1) % 2][:, :, 384:512])
    S.barrier()
    load_own_weights()

    for n in range(NPRE - 1 if NPRE > 0 else 0, NT):
        own = n >= NPRE
        qact = True
        hist_only = not own
        io = n - NPRE
        if not hist_only:
            DMA("sp", xt[:, :], xp[n * 128:(n + 1) * 128, :])
            rstd, nb = lnorm(xt, LN_EPS)
            ACT(xn[:, :], xt[:, :], AF.Identity, bias=nb, scale=rstd)
            mcol = smalls[:, o_mask + n:o_mask + n + 1]
            TS("dve", gm[:, 0:8], lng_c, mcol, None, ALU.mult)
            TS("dve", gm[:, 8:16], lnb_c, mcol, None, ALU.mult)
            ptg = PT.group(8)
            for kc in range(8):
                TR(ptg[kc].v(), xn[:, kc * 128:(kc + 1) * 128])
            for kc in range(8):
                TS("dve", hT[:, kc, :], ptg[kc].v(), gm[:, kc:kc + 1], gm[:, 8 + kc:9 + kc], ALU.mult, ALU.add)

            pba = PQ.get()
            for kc in range(8):
                MM(pba.v(0, 16), hT[:, kc, :], wv("ba", kc, 0, 16), start=(kc == 0), stop=(kc == 7))
            ACT(sm[:, 24:32], pba.v(0, 8), AF.Exp, scale=-1.0)
            TS("dve", sm[:, 24:32], sm[:, 24:32], 1.0, None, ALU.add)
            RECIP(sm[:, 0:8], sm[:, 24:32])
            TS("dve", sm[:, 8:16], sm[:, 0:8], -1.0, None, ALU.mult)
            TT("dve", sm[:, 80:88], pba.v(8, 16), dtb_t, ALU.add)
            ACT(sm[:, 80:88], sm[:, 80:88], AF.Exp)
            ACT(sm[:, 80:88], sm[:, 80:88], AF.Ln, bias=1.0)
            TT("dve", sm[:, 16:24], sm[:, 80:88], negA[:, :], ALU.mult)
            pgc = PQ.get()
            for i, ci in enumerate((C_UBLK, C_BLK, C_MA, C_MB)):
                MM(pgc.v(i * 8, i * 8 + 8), cst(ci), sm[:, 16:24])
            ACT(sm[:, 32:40], pgc.v(0, 8), AF.Exp)
            TT("dve", sm[:, 40:48], sm[:, 32:40], sm[:, 0:8], ALU.mult)
            CP("act", sm[:, 88:96], pgc.v(8, 16))
            TT("dve", sm[:, 80:88], sm[:, 88:96], pgc.v(0, 8), ALU.subtract)
            ACT(sm[:, 80:88], sm[:, 80:88], AF.Exp)
            TS("dve", sm[:, 48:56], sm[:, 80:88], pmA, None, ALU.mult)
            TS("dve", sm[:, 56:64], sm[:, 80:88], pmB, None, ALU.mult)
            ACT(sm[:, 64:80], pgc.v(16, 32), AF.Exp)

        if qact:
            for g in range(4):
                pp = PQ.get()
                for kc in range(8):
                    MM(pp.v(), wv("pl", kc, g * 128, g * 128 + 128), hT[:, kc, :], start=(kc == 0), stop=(kc == 7))
                CP("act", pbuf[g][:, 15:143], pp.v())
        if own:
            for half in range(2):
                pw = PW.get()
                for kc in range(8):
                    MM(pw.v(), hT[:, kc, :], wv("ga", kc, half * 512, half * 512 + 512),
                       start=(kc == 0), stop=(kc == 7))
                if "gate" in FL:
                    SIGM(tmpw[:, half * 512:(half + 1) * 512], pw.v())
                else:
                    ACT(tmpw[:, half * 512:(half + 1) * 512], pw.v(), AF.Sigmoid)
            pya = [PW.get(), PW.get()]
            for g in range(4):
                w = 2 << g
                src = pbuf[g]
                lo = 1
                sh = 1
                cur = src
                k = 0
                while sh < w:
                    dstb = ptmp[k % 2]
                    TT("pool", dstb[:, lo:143], cur[:, lo:143], cur[:, lo - sh:143 - sh], ALU.add)
                    cur = dstb
                    sh *= 2
                    lo = 2 * sh - 1
                    k += 1
                STT("dve", dTp[g][:, :], cur[:, 15:143], 1.0 / w, src[:, 15:143], ALU.mult, ALU.subtract)
                if io == 0:
                    TT("pool", junk[:, 0:16], cur[:, 15:31], smalls[:, o_invc + g * 16:o_invc + g * 16 + 16], ALU.mult)
                    TT("pool", dTp[g][:, 0:16], junk[:, 0:16], src[:, 15:31], ALU.subtract)
                MM(pya[g // 2].v((g % 2) * 256, (g % 2) * 256 + 256), dTp[g][:, :],
                   V(WB["poolw"], WBIG.h[:, off_pw + g * 256:off_pw + g * 256 + 256]))
            for half in range(2):
                TT("dve", yas[:, half * 512:(half + 1) * 512], pya[half].v(), tmpw[:, half * 512:(half + 1) * 512], ALU.mult)
            for half in range(2):
                pw = PW.get()
                for kc in range(8):
                    MM(pw.v(), hT[:, kc, :], wv("zg", kc, half * 512, half * 512 + 512),
                       start=(kc == 0), stop=(kc == 7))
                ACT(tmpw[:, half * 512:(half + 1) * 512], pw.v(), AF.Silu)
        if qact:
            for g in range(4):
                CP("pool", pbuf[g][:, 0:15], pbuf[g][:, 128:143])

        def head_body(h, own=own, qact=qact, io=io, hist_only=hist_only):
            i2 = h % 2
            PQ = PQL[i2]
            PT = PTL[i2]
            newpb = lambda: newpb_l(i2)
            kinds = ("q",) if hist_only else ("k", "v", "q")
            for kind in kinds:
                c = {"k": 0, "v": 8, "q": 16}[kind] + h
                pp = PQ.get()
                for kc in range(8):
                    if kind == "q":
                        rhsw = wv("q", kc, h * 128, h * 128 + 128)
                    else:
                        rhsw = wv("kv", kc, c * 128, c * 128 + 128)
                    MM(pp.v(), rhsw, hT[:, kc, :], start=(kc == 0), stop=(kc == 7))
                prc = pre[c]
                CP("act", prc[:, 3:131], pp.v())
                if hist_only:
                    CP("pool", prc[:, 0:3], prc[:, 128:131])
                    continue
                ca = cacc[c % 2]
                cw = [smalls[:, o_convw + c * 4 + k:o_convw + c * 4 + k + 1] for k in range(4)]
                TS("dve", ca[:, :], prc[:, 3:131], cw[3], None, ALU.mult)
                STT("dve", ca[:, :], prc[:, 2:130], cw[2], ca[:, :], ALU.mult, ALU.add)
                STT("dve", ca[:, :], prc[:, 1:129], cw[1], ca[:, :], ALU.mult, ALU.add)
                STT("dve", ca[:, :], prc[:, 0:128], cw[0], ca[:, :], ALU.mult, ALU.add)
                if SUB >= 3:
                    CP("pool", prc[:, 0:3], prc[:, 128:131])
                if SUB < 4:
                    continue
                if kind == "v":
                    vt = vTb[i2]
                    if "osilu" in FL:
                        SIGM(rnb[c % 2][:, :], ca[:, :])
                        TT("pool", vt[:, :], ca[:, :], rnb[c % 2][:, :], ALU.mult)
                    else:
                        ACT(vt[:, :], ca[:, :], AF.Silu)
                    pt = PT.get()
                    TR(pt.v(), vt[:, :])
                    TS("dve", bv[i2][:, :], pt.v(), sm[:, h:h + 1], None, ALU.mult)
                elif SUB >= 5:
                    ks = ksil[c % 2]
                    if "osilu" in FL:
                        SIGM(rnb[c % 2][:, :], ca[:, :])
                        TT("pool", ks[:, :], ca[:, :], rnb[c % 2][:, :], ALU.mult)
                    else:
                        ACT(ks[:, :], ca[:, :], AF.Silu)
                    kq = ksq[c % 2]
                    ACT(kq[:, :], ks[:, :], AF.Square)
                    pss = PQ.get()
                    MM(pss.v(), onesb[:, :], kq[:, :])
                    rn = rnb[c % 2]
                    if kind == "k":
                        if "orsq" in FL:
                            RSQ(rn[:, :], pss.v(), 1.0, L2_EPS)
                        else:
                            ACT(rn[:, :], pss.v(), AF.Sqrt, bias=L2_EPS)
                            RECIP(rn[:, :], rn[:, :])
                    else:
                        if "orsq" in FL:
                            RSQ(rn[:, :], pss.v(), 128.0, 128.0 * L2_EPS)
                        else:
                            ACT(rn[:, :], pss.v(), AF.Sqrt, bias=128.0 * L2_EPS, scale=128.0)
                            RECIP(rn[:, :], rn[:, :])
                    dst = khatT[i2] if kind == "k" else qhatT[i2]
                    TT("pool", dst[:, :], ks[:, :], rn[:, :], ALU.mult)
                    if kind == "k":
                        pt = PT.get()
                        TR(pt.v(), dst[:, :])
                        TS("dve", kbg[i2][:, :], pt.v(), sm[:, 40 + h:41 + h], None, ALU.mult)
                        ACT(kgA[i2][:, :], pt.v(), AF.Identity, bias=zcol, scale=sm[:, 48 + h:49 + h])
                        ACT(kgB[i2][:, :], pt.v(), AF.Identity, bias=zcol, scale=sm[:, 56 + h:57 + h])

            if hist_only:
                return
            A = Amat[i2]
            ACT(A[:, :], cst(C_UALL), AF.Identity, bias=zcol, scale=sm[:, 16 + h:17 + h])
            pd = PQ.get()
            MM(pd.v(), A[:, :], cst(C_BST))
            ACT(Dex[i2][:, :], pd.v(), AF.Exp)
            TT("pool", Dl[i2][:, :], Dex[i2][:, :], cst(C_MLS), ALU.mult)
            pg = PQ.get()
            MM(pg.v(), khatT[i2][:, :], khatT[i2][:, :])
            P = newpb()
            STT("dve", P[:, :], pg.v(), sm[:, 8 + h:9 + h], Dl[i2][:, :], ALU.mult, ALU.mult)
            pt = PT.get()
            TR(pt.v(), P[:, :])
            PTt = newpb()
            CP("act", PTt[:, :], pt.v())
            TTk = newpb()
            TT("pool", TTk[:, :], identb[:, :], PTt[:, :], ALU.add)
            for lev in range(5):
                pn = PQ.get()
                MM(pn.v(), PTt[:, :], P[:, :])
                Pn = newpb()
                CP("act", Pn[:, :], pn.v())
                if lev < 4:
                    pnt = PQ.get()
                    MM(pnt.v(), P[:, :], PTt[:, :])
                    PnT = newpb()
                    CP("act", PnT[:, :], pnt.v())
                px = PQ.get()
                MM(px.v(), Pn[:, :], TTk[:, :])
                TTn = newpb()
                TT("dve", TTn[:, :], px.v(), TTk[:, :], ALU.add)
                P, TTk = Pn, TTn
                if lev < 4:
                    PTt = PnT
            pu = PQ.get()
            MM(pu.v(), TTk[:, :], bv[i2][:, :])
            u = u32[i2]
            CP("act", u[:, :], pu.v())
            pwt = PQ.get()
            MM(pwt.v(), kbg[i2][:, :], TTk[:, :])
            wT = wTb[i2]
            CP("act", wT[:, :], pwt.v())
            po = None
            if own:
                pdt = PQ.get()
                MM(pdt.v(), cst(C_BST), A[:, :])
                ACT(DTe[i2][:, :], pdt.v(), AF.Exp)
                TT("pool", DTe[i2][:, :], DTe[i2][:, :], cst(C_MUI), ALU.mult)
                pa = PQ.get()
                MM(pa.v(), khatT[i2][:, :], qhatT[i2][:, :])
                TT("dve", attnT[i2][:, :], pa.v(), DTe[i2][:, :], ALU.mult)
                ACT(Grep[i2][:, :], cst(C_ONE), AF.Identity, bias=zcol, scale=sm[:, 16 + h:17 + h])
                pe_ = PQ.get()
                MM(pe_.v(), Grep[i2][:, :], cst(C_UBLK))
                ACT(Ebc[i2][:, :], pe_.v(), AF.Exp)
                TT("pool", qgTa[i2][:, 0:64], qhatT[i2][:, 0:64], Ebc[i2][:, 0:64], ALU.mult)
                TT("pool", qgTb[i2][:, 64:128], qhatT[i2][:, 64:128], Ebc[i2][:, 64:128], ALU.mult)
                po = POL[i2].get()
                MM(po.v(), qgTa[i2][:, :], Sbf[h][:, :], start=True, stop=False, hold=False)
            vn = vnew[i2]
            pws = PQ.get()
            MM(pws.v(), wT[:, :], Sbf[h][:, :])
            TT("dve", vn[0:64, :], u[0:64, :], pws.v(r0=0, r1=64), ALU.subtract)
            psu = PQ.get()
            MM(psu.v(), kgA[i2][:, :], vn[:, :])
            STT("dve", Sbf[h][:, :], S32[h][:, :], sm[:, 64 + h:65 + h], psu.v(), ALU.mult, ALU.add)
            STT("dve", S32[h][:, :], S32[h][:, :], sm[:, 64 + h:65 + h], psu.v(), ALU.mult, ALU.add)
            if own:
                MM(po.v(), qgTb[i2][:, :], Sbf[h][:, :], start=False, stop=False, hold=False)
            pws2 = PQ.get()
            MM(pws2.v(), wT[:, :], Sbf[h][:, :])
            TT("dve", vn[64:128, :], u[64:128, :], pws2.v(r0=64, r1=128), ALU.subtract)
            psu2 = PQ.get()
            MM(psu2.v(), kgB[i2][:, :], vn[:, :])
            STT("dve", Sbf[h][:, :], S32[h][:, :], sm[:, 72 + h:73 + h], psu2.v(), ALU.mult, ALU.add)
            STT("dve", S32[h][:, :], S32[h][:, :], sm[:, 72 + h:73 + h], psu2.v(), ALU.mult, ALU.add)
            if own:
                MM(po.v(), attnT[i2][:, :], vn[:, :], start=False, stop=True)
                ACT(junk[:, :], po.v(), AF.Square, accum=rs[:, h:h + 1])
                if "orsq" in FL:
                    RSQ(rs[:, 8 + h:9 + h], rs[:, h:h + 1], 1.0 / 128.0, RMS_EPS)
                else:
                    ACT(rs[:, 8 + h:9 + h], rs[:, h:h + 1], AF.Sqrt, bias=RMS_EPS, scale=1.0 / 128.0)
                    RECIP(rs[:, 8 + h:9 + h], rs[:, 8 + h:9 + h])
                STT("dve", ybn[:, h * 128:(h + 1) * 128], po.v(), rs[:, 8 + h:9 + h], onw_t[:, :], ALU.mult, ALU.mult)

        for h0 in range(0, NH if LVL >= 5 else 0, 2):
            lists = []
            for h in (h0, h0 + 1):
                S.capture = []
                head_body(h)
                lists.append(S.capture)
                S.capture = None
            S.replay(lists, chunk=1)

        if own and LVL >= 7:
            for half in range(2):
                hs = slice(half * 512, (half + 1) * 512)
                TT("dve", ybn[:, hs], ybn[:, hs], tmpw[:, hs], ALU.mult)
                pw = PW.get()
                for kc in range(8):
                    MM(pw.v(), hT[:, kc, :], wv("zg", kc, 1024 + half * 512, 1024 + half * 512 + 512),
                       start=(kc == 0), stop=(kc == 7))
                ACT(tmpw[:, hs], pw.v(), AF.Sigmoid)
                TT("dve", ybn[:, hs], ybn[:, hs], tmpw[:, hs], ALU.mult)
            TT("dve", mixb[:, :], yas[:, :], ybn[:, :], ALU.add)
            DMA("sp", tmpw[:, :], tabd(T_G0))
            DMA("sp", ybn[:, :], tabd(T_B0))
            ptg = PT.group(8)
            for kc in range(8):
                TR(ptg[kc].v(), mixb[:, kc * 128:(kc + 1) * 128])
            for kc in range(8):
                CP("act" if kc % 2 else "dve", mixT[:, kc, :], ptg[kc].v())
            htok = yas
            ACT(htok[:, :], xt[:, :], AF.Identity, bias=nb, scale=rstd)
            TT("dve", htok[:, :], htok[:, :], tmpw[:, :], ALU.mult)
            TT("dve", htok[:, :], htok[:, :], ybn[:, :], ALU.add)
            DMA("sp", tmpw[:, :], tabd(T_G1))
            r1 = ybn
            for half in range(2):
                pw = PW.get()
                for kc in range(8):
                    MM(pw.v(), mixT[:, kc, :], wv("out", kc, half * 512, half * 512 + 512),
                       start=(kc == 0), stop=(kc == 7))
                STT("dve", r1[:, half * 512:(half + 1) * 512], htok[:, half * 512:(half + 1) * 512], ALPHA,
                    pw.v(), ALU.mult, ALU.add)
            rstd1, nb1 = lnorm(r1, LN_EPS)
            h1 = yas
            ACT(h1[:, :], r1[:, :], AF.Identity, bias=nb1, scale=rstd1)
            DMA("sp", ybn[:, :], tabd(T_B1))
            TT("dve", h1[:, :], h1[:, :], tmpw[:, :], ALU.mult)
            TT("dve", h1[:, :], h1[:, :], ybn[:, :], ALU.add)
            a0 = ybn
            TS("dve", a0[:, :], h1[:, :], ALPHA, None, ALU.mult)
            DMA("sp", acc0_d[io * 128:(io + 1) * 128, :], a0[:, :], key="d:acc0w")
            if debug:
                DMA("sp", dbg_d[io * 128:(io + 1) * 128, :], h1[:, :], key="d:dbgw")
            CP("act", mixb[:, :], h1[:, :])
            ptg = PT.group(8)
            for kc in range(8):
                TR(ptg[kc].v(), mixb[:, kc * 128:(kc + 1) * 128])
            for kc in range(8):
                CP("act" if kc % 2 else "dve", mixT[:, kc, :], ptg[kc].v())
            DMA("sp", V(Buf("h1Tw%d" % io), h1T_d[:, :, io * 128:(io + 1) * 128]), mixT[:, :, :], key="d:h1Tw")

    S.barrier()

    h1Tg = wview(0, [128, 8, 512], BF16, "h1Tg")
    wup = [wview(4096 + i * 4096, [128, 8, 512], BF16, "wup%d" % i) for i in range(2)]
    wdn = [wview(12288 + i * 4096, [128, 4, 1024], BF16, "wdn%d" % i) for i in range(2)]
    actT = [wview(20480 + i * 2048, [128, 4, 512], BF16, "actT%d" % i) for i in range(2)]
    Wg = wview(24576, [128, 8, 1024], BF16, "Wg")
    Wp = wview(32768, [128, 2, 1024], BF16, "Wp")
    acc = wview(34816, [128, 4, 1024], F32, "acc")
    rTt = wview(43008, [128, 8, 128], BF16, "rTt")
    pTt = wview(44032, [128, 2, 128], BF16, "pTt")
    pbf = wview(44288, [128, 256], BF16, "pbf")
    ln2t = wview(44544, [128, 2, 1024], F32, "ln2t")
    DMA("pool", Wg[:, :, :], w_g_d.rearrange("(k p) c -> p k c", p=128))
    DMA("pool", Wp[:, :, :], w_p_d.rearrange("(k p) c -> p k c", p=128))
    DMA("sp", ln2t[:, 0, :], tabd(T_G2))
    DMA("sp", ln2t[:, 1, :], tabd(T_B2))
    h1Tg2 = [h1Tg, wview(48640, [128, 8, 512], BF16, "h1Tg_b")]
    acc2 = [acc, wview(52736, [128, 4, 1024], F32, "acc_b")]
    ptl = wview(60928, [128, 256], F32, "ptl")
    relu_t = [ybn, yas]
    NFB = DFF // 512
    blkc = [0]
    PWF = RR([wbanks[0], wbanks[1]])
    PWP = RR([wbanks[2], wbanks[3]])

    class _TV:
        def __init__(self, h, b_):
            self.h, self.b = h, b_

        def __getitem__(self, idx):
            return V(self.b, self.h[idx])

    def ffn_core(G):
        hg = h1Tg2[G % 2]
        ac = acc2[G % 2]
        DMA("sp", hg[:, :, :], h1T_d[:, :, G * 512:(G + 1) * 512])
        DMA("sp", ac[:, :, :], acc0_d[G * 512:(G + 1) * 512, :].rearrange("(t p) d -> p t d", p=128))
        for fb in range(NFB):
            wu = wup[blkc[0] % 2]
            wd = wdn[blkc[0] % 2]
            at = actT[blkc[0] % 2]
            blkc[0] += 1
            DMA("pool", wu[:, :, :], w_up_d[:, fb * 512:(fb + 1) * 512].rearrange("(k p) c -> p k c", p=128))
            DMA("pool", wd[:, :, :], w_down_d[fb * 512:(fb + 1) * 512, :].rearrange("(k p) c -> p k c", p=128))
            for fc in range(4):
                S.begin_atomic()
                pw = PWF.get()
                for kc in range(8):
                    MM(pw.v(), wu[:, kc, fc * 128:(fc + 1) * 128], hg[:, kc, :], start=(kc == 0), stop=(kc == 7))
                rt = relu_t[fc % 2]
                ACT(rt[:, 0:512], pw.v(), AF.Relu)
                S.end_atomic()
                TT("pool", at[:, fc, :], rt[:, 0:512], rt[:, 0:512], ALU.mult)
            for tt in range(4):
                for half in range(2):
                    S.begin_atomic()
                    pw = PWF.get()
                    for fc in range(4):
                        MM(pw.v(), at[:, fc, tt * 128:(tt + 1) * 128], wd[:, fc, half * 512:(half + 1) * 512],
                           start=(fc == 0), stop=(fc == 3))
                    TT("dve", ac[:, tt, half * 512:(half + 1) * 512], ac[:, tt, half * 512:(half + 1) * 512],
                       pw.v(), ALU.add)
                    S.end_atomic()

    def ple_stage(G):
        ac = acc2[G % 2]
        for tt in range(4):
            io = G * 4 + tt
            CP("act", mixb[:, :], ac[:, tt, :])
            S.begin_atomic()
            ptg = PT.group(8)
            for kc in range(8):
                TR(ptg[kc].v(), mixb[:, kc * 128:(kc + 1) * 128])
            for kc in range(8):
                CP("act" if kc % 2 else "dve", rTt[:, kc, :], ptg[kc].v())
            S.end_atomic()
            DMA("sp", ptl[:, :], p_own[io * 128:(io + 1) * 128, :])
            CP("act", pbf[:, :], ptl[:, :])
            S.begin_atomic()
            ptg = PT.group(2)
            for c in range(2):
                TR(ptg[c].v(), pbf[:, c * 128:(c + 1) * 128])
            for c in range(2):
                CP("act", pTt[:, c, :], ptg[c].v())
            S.end_atomic()
            sg = tmpw
            for half in range(2):
                S.begin_atomic()
                pw = PWP.get()
                for kc in range(8):
                    MM(pw.v(), rTt[:, kc, :], Wg[:, kc, half * 512:(half + 1) * 512], start=(kc == 0), stop=(kc == 7))
                ACT(sg[:, half * 512:(half + 1) * 512], pw.v(), AF.Sigmoid)
                S.end_atomic()
                S.begin_atomic()
                pw2 = PWP.get()
                for c in range(2):
                    MM(pw2.v(), pTt[:, c, :], Wp[:, c, half * 512:(half + 1) * 512], start=(c == 0), stop=(c == 1))
                TT("dve", sg[:, half * 512:(half + 1) * 512], sg[:, half * 512:(half + 1) * 512], pw2.v(), ALU.mult)
                S.end_atomic()
            r2 = _TV(ac.h[:, tt, :], ac.b)
            TT("dve", r2[:, :], sg[:, :], r2[:, :], ALU.add)
            rstd2, nb2 = lnorm(r2, LN_EPS)
            h2 = tmpw
            ACT(h2[:, :], r2[:, :], AF.Identity, bias=nb2, scale=rstd2)
            TT("dve", h2[:, :], h2[:, :], ln2t[:, 0, :], ALU.mult)
            TT("dve", h2[:, :], h2[:, :], ln2t[:, 1, :], ALU.add)
            DMA("sp", V(Buf("outw%d" % io), out_d[io * 128:(io + 1) * 128, :]), h2[:, :], key="d:outw")

    def capf(fn, *args):
        S.capture = []
        fn(*args)
        l = S.capture
        S.capture = None
        return l

    NGe = NG if LVL >= 8 else 0
    if NGe > 0:
        ffn_core(0)
    for G in range(NGe):
        lists = [capf(ple_stage, G)]
        if G + 1 < NGe:
            lists.append(capf(ffn_core, G + 1))
        S.replay(lists, chunk=1)

    fin = [(k, v) for k, v in S.dma_cnt.items()]
    S.wait_all("sp", fin)
    S.emit()
    return nc, used[0], S.n_sems


def make_consts():
    idx = np.arange(128)
    m, i = idx[:, None], idx[None, :]
    same = (m // 64) == (i // 64)
    c = np.zeros((10, 128, 128), np.float32)
    c[0] = np.eye(128)
    c[1] = (m <= i) & same
    c[2] = same
    c[3] = (m < 64) & (i >= 0)
    c[4] = (m >= 64) & (i >= 0)
    c[5] = (m <= i)
    c[6] = (m > i)
    c[7] = (m > i) & same
    c[8] = (m <= i) & same
    c[9] = 1.0
    return np.ascontiguousarray(c.transpose(1, 0, 2).reshape(128, 10 * 128))


def core_inputs(inp, b, j, NPRE, NOWN, seq):
    NT = NPRE + NOWN
    x = np.asarray(inp["x"], np.float32)
    own0 = j * NOWN * 128
    start = own0 - NPRE * 128
    xp = np.zeros((NT * 128, D), np.float32)
    lo = max(start, 0)
    xp[lo - start:] = x[b, lo:own0 + NOWN * 128]
    mask = np.zeros(NT, np.float32)
    mask[(lo - start) // 128:] = 1.0
    w_in = np.asarray(inp["w_in"], np.float32)[0]
    P0 = 512
    w_pl = w_in[:, 0:512]
    w_q = w_in[:, P0:P0 + 1024]
    w_k = w_in[:, P0 + 1024:P0 + 2048]
    w_v = w_in[:, P0 + 2048:P0 + 3072]
    w_z = w_in[:, 3584:4608]
    w_beta = w_in[:, 4608:4616]
    w_a = w_in[:, 4616:4624]
    w_ga = w_in[:, 4624:5648]
    w_gb = w_in[:, 5648:6672]
    conv_w = np.asarray(inp["conv_w"], np.float32)[0]
    cw = conv_w.T
    cw = np.concatenate([cw[1024:2048], cw[2048:3072], cw[0:1024]], 0)
    cw = cw.reshape(24, 128, 4).transpose(1, 0, 2).reshape(128, 96)
    NSM = 8 + 8 + NT + 16 + 16 + 64 + 2 + 96
    sm = np.zeros((128, NSM), np.float32)
    o = 0
    sm[:, o:o + 8] = np.asarray(inp["ln_in_g"], np.float32).reshape(8, 128).T; o += 8
    sm[:, o:o + 8] = np.asarray(inp["ln_in_b"], np.float32).reshape(8, 128).T; o += 8
    sm[:, o:o + NT] = mask[None, :]; o += NT
    sm[:, o:o + 8] = np.asarray(inp["a_log"], np.float32)[0][None, :]
    sm[:, o + 8:o + 16] = np.asarray(inp["dt_bias"], np.float32)[0][None, :]; o += 16
    o += 16
    invc = np.zeros((4, 16), np.float32)
    for g, w in enumerate((2, 4, 8, 16)):
        t = own0 + np.arange(16)
        invc[g] = 1.0 / np.minimum(t + 1, w)
    sm[:, o:o + 64] = invc.reshape(1, 64); o += 64
    sm[:64, o] = 1.0
    sm[64:, o + 1] = 1.0; o += 2
    sm[:, o:o + 96] = cw; o += 96
    tabs = np.zeros((128, 7 * D + 128), np.float32)
    for i, nm in enumerate(("ln_in_g", "ln_in_b", "pool_scale", "ln1_g", "ln1_b", "ln2_g", "ln2_b")):
        tabs[:, i * D:(i + 1) * D] = np.asarray(inp[nm], np.float32).reshape(-1)[None, :]
    tabs[:, 7 * D:] = np.asarray(inp["o_norm_w"], np.float32).reshape(-1)[None, :]
    poolw = np.asarray(inp["pool_w"], np.float32)[0]
    poolw = np.ascontiguousarray(poolw.transpose(1, 0, 2).reshape(128, 1024))
    c = np.ascontiguousarray
    return {
        "xp": xp,
        "p_own": c(np.asarray(inp["p"], np.float32)[0, b, own0:own0 + NOWN * 128]),
        "consts": make_consts(),
        "smalls": sm,
        "tabs": tabs,
        "w_kv": c(np.concatenate([w_k, w_v], 1)),
        "w_q": c(w_q),
        "w_zg": c(np.concatenate([w_z, w_gb], 1)),
        "w_ga": c(w_ga),
        "w_ba": c(np.concatenate([w_beta, w_a], 1)),
        "w_pl": c(w_pl),
        "poolw": poolw,
        "w_out": c(np.asarray(inp["w_out"], np.float32)[0]),
        "w_up": c(np.asarray(inp["w_up"], np.float32)[0]),
        "w_down": c(np.asarray(inp["w_down"], np.float32)[0]),
        "w_g": c(np.asarray(inp["ple_gate_w"], np.float32)[0]),
        "w_p": c(np.asarray(inp["ple_proj_w"], np.float32)[0]),
    }


_NC_CACHE = {}


def kernel(**inputs):
    NPRE, NOWN = 48, 16
    x = np.asarray(inputs["x"])
    B, SEQ, _ = x.shape
    key = (NPRE, NOWN)
    if key not in _NC_CACHE:
        _NC_CACHE[key] = build(NPRE, NOWN)[0]
    nc = _NC_CACHE[key]
    in_maps = []
    for c in range(8):
        in_maps.append(core_inputs(inputs, c // 4, c % 4, NPRE, NOWN, SEQ))
    res = run_bass_kernel_spmd(nc, in_maps, core_ids=list(range(8)))
    out = np.zeros((B, SEQ, D), np.float32)
    for c in range(8):
        b, j = c // 4, c % 4
        out[b, j * NOWN * 128:(j + 1) * NOWN * 128] = res.results[c]["out"]
    return out
```

```python
import numpy as np
import concourse.bass as bass
import concourse.mybir as mybir
from concourse.bass_utils import run_bass_kernel_spmd

F32 = mybir.dt.float32
BF16 = mybir.dt.bfloat16
ALU = mybir.AluOpType
AF = mybir.ActivationFunctionType

D = 1024
NH = 8
DFF = 4096
PLE = 256
ALPHA = 2.0 ** 0.25
LN_EPS = 1e-5
RMS_EPS = 1e-6
L2_EPS = 1e-6
ENGS = ("pe", "act", "dve", "pool", "sp")
SEM_LIMIT = 12000


class Buf:
    __slots__ = ("name", "w", "rs", "tr", "te", "tl")

    def __init__(self, name):
        self.name = name
        self.w = None
        self.rs = []
        self.tr = 0.0
        self.te = None
        self.tl = 0.0


class V:
    __slots__ = ("b", "ap")

    def __init__(self, b, ap):
        self.b = b
        self.ap = ap


class TB:
    def __init__(self, h, name):
        self.h = h
        self.b = Buf(name)

    def __getitem__(self, idx):
        return V(self.b, self.h[idx])


class Sched:
    def __init__(self, nc):
        self.nc = nc
        self.ops = {e: [] for e in ENGS}
        self.cnt = {e: 0 for e in ENGS}
        self.epoch = {e: 0 for e in ENGS}
        self.sems = {}
        self.known = {e: {} for e in ENGS}
        self.dma_cnt = {}
        self.n_sems = 0
        self.capture = None
        self.atomic = False
        self.tm = {e: 0.0 for e in ENGS}

    def _sem(self, key):
        if key not in self.sems:
            self.sems[key] = self.nc.alloc_semaphore("s_" + str(key).replace(":", "_"))
            self.n_sems += 1
        return self.sems[key]

    def _collect(self, eng, reads, writes):
        need = {}

        def add(tok):
            if tok is None:
                return
            if need.get(tok[0], 0) < tok[1]:
                need[tok[0]] = tok[1]
        for b in reads:
            add(b.w)
        for b in writes:
            add(b.w)
            for r in b.rs:
                add(r)
        kn = self.known[eng]
        out = []
        for k, v in need.items():
            if kn.get(k, 0) >= v:
                continue
            if eng == "pe" and k[0] == "pe" and isinstance(k, tuple):
                continue
            kn[k] = v
            out.append((k, v))
        return out

    @staticmethod
    def _mark(tok, reads, writes):
        for b in reads:
            b.rs.append(tok)
        for b in writes:
            b.w = tok
            b.rs = []

    XLAT = 0.7

    def _est(self, eng, reads, writes):
        t = self.tm[eng]
        for b in reads:
            t = max(t, b.tr + (self.XLAT if b.te != eng else 0.05))
        for b in writes:
            t = max(t, b.tr + (self.XLAT if b.te != eng else 0.05), b.tl + self.XLAT)
        return t

    def _commit(self, eng, reads, writes, cost, lat=0.0):
        st = self._est(eng, reads, writes)
        en = st + cost
        self.tm[eng] = en
        for b in reads:
            if b.tl < en + lat:
                b.tl = en + lat
        for b in writes:
            b.tr = en + lat
            b.te = eng
            b.tl = 0.0

    def begin_atomic(self):
        self.atomic = True

    def end_atomic(self):
        self.atomic = False
        if self.capture:
            t = self.capture[-1]
            self.capture[-1] = t[:6] + (False,) + t[7:]

    def op(self, eng, fn, reads=(), writes=(), hold=False, cost=None):
        if cost is None:
            cost = {"pe": 0.2, "act": 0.35, "dve": 0.3, "pool": 0.8, "sp": 0.05}[eng]
        if self.capture is not None:
            self.capture.append((0, eng, fn, list(reads), list(writes), None, hold or self.atomic, cost))
            return None
        self._commit(eng, reads, writes, cost)
        waits = self._collect(eng, reads, writes)
        if self.cnt[eng] >= SEM_LIMIT:
            self.epoch[eng] += 1
            self.cnt[eng] = 0
        self.cnt[eng] += 1
        key = (eng, self.epoch[eng])
        self._sem(key)
        tok = (key, self.cnt[eng])
        self.ops[eng].append((waits, fn, (key, 1)))
        self._mark(tok, reads, writes)
        return tok

    def dma(self, eng, fn, reads=(), writes=(), key=None):
        if self.capture is not None:
            self.capture.append((1, eng, fn, list(reads), list(writes), key, False, 0.05))
            return None
        self._commit(eng, reads, writes, 0.05, lat=2.0)
        waits = self._collect(eng, reads, writes)
        if key is None:
            key = "d:" + (writes[0].name if writes else reads[0].name)
        self._sem(key)
        self.dma_cnt[key] = self.dma_cnt.get(key, 0) + 16
        tok = (key, self.dma_cnt[key])
        self.ops[eng].append((waits, fn, (key, 16)))
        self._mark(tok, reads, writes)
        return tok

    def replay(self, lists, chunk=1):
        idx = [0] * len(lists)
        lists = [l for l in lists if l]
        idx = [0] * len(lists)
        while True:
            best, bt = None, None
            for li, l in enumerate(lists):
                if idx[li] < len(l):
                    k, eng, fn, r, w, key, hold, cost = l[idx[li]]
                    t = (self._est(eng, r, w), idx[li] / len(l))
                    if bt is None or t < bt:
                        best, bt = li, t
            if best is None:
                break
            l = lists[best]
            n_em = 0
            while idx[best] < len(l):
                k, eng, fn, r, w, key, hold, cost = l[idx[best]]
                idx[best] += 1
                n_em += 1
                if k == 0:
                    self.op(eng, fn, r, w, cost=cost)
                else:
                    self.dma(eng, fn, r, w, key)
                if not hold and n_em >= chunk:
                    break

    def wait_all(self, eng, toks):
        need = {}
        for t in toks:
            if t is not None and need.get(t[0], 0) < t[1]:
                need[t[0]] = t[1]
        self.ops[eng].append((list(need.items()), None, None))

    def barrier(self):
        toks = []
        for e in ENGS:
            if self.cnt[e] > 0:
                toks.append(((e, self.epoch[e]), self.cnt[e]))
        for k, v in self.dma_cnt.items():
            toks.append((k, v))
        for e in ENGS:
            self.wait_all(e, toks)
            for k, v in toks:
                if self.known[e].get(k, 0) < v:
                    self.known[e][k] = v

    def emit(self):
        nc = self.nc
        handles = {"pe": "tensor", "act": "scalar", "dve": "vector",
                   "pool": "gpsimd", "sp": "sync"}
        with nc.Block() as block:
            for e in ENGS:
                ops = self.ops[e]

                def body(engh, ops=ops):
                    for waits, fn, inc in ops:
                        for k, v in waits:
                            engh.wait_ge(self.sems[k], v)
                        if fn is None:
                            continue
                        ins = fn(engh)
                        ins.then_inc(self.sems[inc[0]], inc[1])
                getattr(block, handles[e])(body)


def build(NPRE, NOWN, debug=False, LVL=9, SUB=9, SIGM_OLD=False, RSQ_OLD=False, FL=("ln", "psilu", "osilu", "orsq")):
    NT = NPRE + NOWN
    NG = NOWN // 4
    nc = bass.Bass("TRN2", target_bir_lowering=False)
    S = Sched(nc)

    def din(name, shape, dt=F32):
        return nc.dram_tensor(name, list(shape), dt, kind="ExternalInput").ap()

    xp = din("xp", [NT * 128, D])
    p_own = din("p_own", [NOWN * 128, PLE])
    consts_d = din("consts", [128, 10 * 128])
    smalls_d = din("smalls", [128, 8 + 8 + NT + 16 + 16 + 64 + 2 + 96])
    tabs_d = din("tabs", [128, 7 * D + 128])
    w_kv_d = din("w_kv", [D, 2048])
    w_q_d = din("w_q", [D, 1024])
    w_zg_d = din("w_zg", [D, 2048])
    w_ga_d = din("w_ga", [D, 1024])
    w_ba_d = din("w_ba", [D, 16])
    w_pl_d = din("w_pl", [D, 512])
    poolw_d = din("poolw", [128, 4 * 256])
    w_out_d = din("w_out", [D, D])
    w_up_d = din("w_up", [D, DFF])
    w_down_d = din("w_down", [DFF, D])
    w_g_d = din("w_g", [D, D])
    w_p_d = din("w_p", [PLE, D])
    out_d = nc.dram_tensor("out", [NOWN * 128, D], F32, kind="ExternalOutput").ap()
    acc0_d = nc.dram_tensor("acc0_scr", [NOWN * 128, D], F32, kind="Internal").ap()
    h1T_d = nc.dram_tensor("h1T_scr", [128, 8, NOWN * 128], BF16, kind="Internal").ap()
    dbg_d = None
    if debug:
        dbg_d = nc.dram_tensor("dbg", [NOWN * 128, D], F32, kind="ExternalOutput").ap()

    used = [0]

    def sb(name, shape, dt=F32):
        n = 1
        for s in shape[1:]:
            n *= s
        used[0] += n * (4 if dt == F32 else 2)
        return TB(nc.alloc_sbuf_tensor("sb_" + name, list(shape), dt), name)

    def bufs(vs):
        return [v.b for v in vs if isinstance(v, V)]

    def apof(x):
        return x.ap if isinstance(x, V) else x

    def fsz(ap):
        n = 1
        for d_ in ap.shape[1:]:
            n *= d_
        return n

    def MM(o, l, r, start=True, stop=True, hold=None):
        c = 0.16 + fsz(r.ap) * (4 if l.ap.dtype == F32 else 1) / 2400.0
        S.op("pe", lambda e: e.matmul(o.ap, lhsT=l.ap, rhs=r.ap, start=start, stop=stop),
             reads=[l.b, r.b], writes=[o.b], hold=((not stop) if hold is None else hold), cost=c)

    def TR(o, i):
        S.op("pe", lambda e: e.transpose(o.ap, i.ap, identb[:, :].ap),
             reads=[i.b, identb.b], writes=[o.b])

    def ACT(o, i, func, bias=None, scale=None, accum=None):
        kw = {}
        if bias is not None:
            kw["bias"] = apof(bias)
        if scale is not None:
            kw["scale"] = apof(scale)
        if accum is not None:
            kw["accum_out"] = accum.ap
        w = [o.b] + ([accum.b] if accum is not None else [])
        S.op("act", lambda e: e.activation(out=o.ap, in_=i.ap, func=func, **kw),
             reads=bufs([i, bias, scale]), writes=w, cost=0.22 + fsz(o.ap) / 1200.0)

    def TS(eng, o, i, s1, s2, op0, op1=None):
        if op1 is None:
            S.op(eng, lambda e: e.tensor_scalar(out=o.ap, in0=i.ap, scalar1=apof(s1), scalar2=None, op0=op0),
                 reads=bufs([i, s1]), writes=[o.b], cost=(0.14 + fsz(o.ap) / 960.0) if eng == "dve" else (0.2 + fsz(o.ap) / 400.0))
        else:
            S.op(eng, lambda e: e.tensor_scalar(out=o.ap, in0=i.ap, scalar1=apof(s1), scalar2=apof(s2),
                                                op0=op0, op1=op1),
                 reads=bufs([i, s1, s2]), writes=[o.b], cost=(0.14 + fsz(o.ap) / 960.0) if eng == "dve" else (0.2 + fsz(o.ap) / 400.0))

    def TT(eng, o, a, b, op):
        S.op(eng, lambda e: e.tensor_tensor(out=o.ap, in0=a.ap, in1=b.ap, op=op),
             reads=[a.b, b.b], writes=[o.b], cost=(0.14 + fsz(o.ap) / 960.0) if eng == "dve" else (0.2 + fsz(o.ap) / 400.0))

    def STT(eng, o, a, s, b, op0, op1):
        S.op(eng, lambda e: e.scalar_tensor_tensor(out=o.ap, in0=a.ap, scalar=apof(s), in1=b.ap, op0=op0, op1=op1),
             reads=bufs([a, s, b]), writes=[o.b], cost=(0.14 + fsz(o.ap) / 960.0) if eng == "dve" else (0.2 + fsz(o.ap) / 400.0))

    def CP(eng, o, i):
        if eng == "act":
            S.op("act", lambda e: e.copy(out=o.ap, in_=i.ap), reads=[i.b], writes=[o.b],
                 cost=0.22 + fsz(o.ap) / 1200.0)
        else:
            S.op(eng, lambda e: e.tensor_copy(out=o.ap, in_=i.ap), reads=[i.b], writes=[o.b], cost=(0.14 + fsz(o.ap) / 960.0) if eng == "dve" else (0.2 + fsz(o.ap) / 400.0))

    def MSET(eng, o, val):
        S.op(eng, lambda e: e.memset(o.ap, val), reads=[], writes=[o.b])

    def DMA(eng, o, i, key=None):
        return S.dma(eng, lambda e: e.dma_start(out=apof(o), in_=apof(i)),
                     reads=bufs([i]), writes=bufs([o]), key=key)

    def SIGM(o, i):
        if SIGM_OLD:
            ACT(o, i, AF.Sigmoid)
            return
        ACT(o, i, AF.Exp, scale=-1.0)
        ACT(o, o, AF.Ln, bias=1.0)
        ACT(o, o, AF.Exp, scale=-1.0)

    def RSQ(o, i, scale, bias):
        if RSQ_OLD:
            ACT(o, i, AF.Sqrt, bias=bias, scale=scale)
            RECIP(o, o)
            return
        ACT(o, i, AF.Ln, bias=bias, scale=scale)
        ACT(o, o, AF.Exp, scale=-0.5)

    def RECIP(o, i):
        S.op("dve", lambda e: e.reciprocal(out=o.ap, in_=i.ap), reads=[i.b], writes=[o.b],
             cost=0.1 + fsz(o.ap) / 400.0)

    class Slot:
        def __init__(self, h, c0, w, b):
            self.h, self.c0, self.w, self.b = h, c0, w, b

        def v(self, c0=0, c1=None, r0=0, r1=128):
            c1 = self.w if c1 is None else c1
            return V(self.b, self.h[r0:r1, self.c0 + c0:self.c0 + c1])

    class RR:
        def __init__(self, banks):
            self.banks, self.i = banks, 0

        def get(self):
            return self.group(1)[0]

        def group(self, n):
            bk = self.banks[self.i % len(self.banks)]
            self.i += 1
            return bk[:n]

        def get_bank(self):
            return self.group(len(self.banks[0]))

    def mkbank(name, dt, ncols, w):
        hq = nc.alloc_psum_tensor(name, [128, ncols], dt)
        b = Buf(name)
        return [Slot(hq, i * w, w, b) for i in range(ncols // w)]
    ptbanks = [mkbank("ptb%d" % i, BF16, 1024, 128) for i in range(2)]
    PT = RR(ptbanks)
    PTL = [RR([ptbanks[0]]), RR([ptbanks[1]])]
    fbanks = [mkbank("pf%d" % i, F32, 512, 128) for i in range(4)]
    wbanks = [[Slot(bk[0].h, 0, 512, bk[0].b)] for bk in fbanks]
    PQ = RR([fbanks[0], fbanks[2], fbanks[1], fbanks[3]])
    PQL = [RR([fbanks[0], fbanks[1]]), RR([fbanks[2], fbanks[3]])]
    pobanks = [mkbank("po%d" % i, F32, 512, 128) for i in range(2)]
    POL = [RR([pobanks[0]]), RR([pobanks[1]])]
    PW = RR([wbanks[0], wbanks[2], wbanks[1], wbanks[3]])

    consts = sb("consts", [128, 10 * 128])
    (C_ID, C_UBLK, C_BLK, C_MA, C_MB, C_UALL, C_BST, C_MLS, C_MUI, C_ONE) = range(10)

    def cst(i):
        return consts[:, i * 128:(i + 1) * 128]
    identb = sb("identb", [128, 128], BF16)
    onesb = sb("onesb", [128, 128], BF16)
    NSM = 8 + 8 + NT + 16 + 16 + 64 + 2 + 96
    smalls = sb("smalls", [128, NSM])
    o = 0
    lng_c = smalls[:, o:o + 8]; o += 8
    lnb_c = smalls[:, o:o + 8]; o += 8
    o_mask = o; o += NT
    alog_t = smalls[:, o:o + 8]; dtb_t = smalls[:, o + 8:o + 16]; o += 16
    zcol = smalls[:, o:o + 1]
    o += 16
    o_invc = o; o += 64
    pmA = smalls[:, o:o + 1]; pmB = smalls[:, o + 1:o + 2]; o += 2
    o_convw = o; o += 96
    (T_G0, T_B0, T_PS, T_G1, T_B1, T_G2, T_B2) = range(7)

    def tabd(i):
        return tabs_d[:, i * D:(i + 1) * D]
    onw_t = sb("onw", [128, 128])
    negA = sb("negA", [128, 8])
    tmpw = sb("tmpw", [128, D])
    ybn = sb("ybn", [128, D])
    yas = sb("yas", [128, D])

    WOFF = {}
    woff = 0
    for nm, ncols in (("kv", 2048), ("q", 1024), ("zg", 2048), ("ga", 1024), ("ba", 16),
                      ("pl", 512), ("out", 1024)):
        WOFF[nm] = (woff, ncols)
        woff += 8 * ncols
    WOFF["poolw"] = (woff, 1024)
    woff += 1024
    WBIG = sb("wbig", [128, woff], BF16)

    WB = {nm: Buf("w_" + nm) for nm in WOFF}

    def wv(nm, kc, c0, c1):
        off, ncols = WOFF[nm]
        return V(WB[nm], WBIG.h[:, off + kc * ncols + c0: off + kc * ncols + c1])

    def wload(nm, src):
        off, ncols = WOFF[nm]
        dst = WBIG.h[:, off:off + 8 * ncols].rearrange("p (k c) -> p k c", k=8)
        DMA("pool", V(WB[nm], dst), src.rearrange("(k p) c -> p k c", p=128))

    def wview(off, shape, dt, name):
        n = 1
        for s in shape[1:]:
            n *= s
        ap = WBIG.h[:, off:off + n * (2 if dt == F32 else 1)]
        if dt == F32:
            ap = ap.bitcast(F32)
        if len(shape) == 3:
            ap = ap.rearrange("p (k c) -> p k c", k=shape[1])
        t = TB.__new__(TB)
        t.h = ap
        t.b = Buf(name)
        return t

    DMA("sp", consts[:, :], consts_d)
    DMA("sp", smalls[:, :], smalls_d)
    DMA("sp", onw_t[:, :], tabs_d[:, 7 * D:7 * D + 128])
    DMA("pool", identb[:, :], consts_d[:, 0:128])
    DMA("pool", onesb[:, :], consts_d[:, C_ONE * 128:(C_ONE + 1) * 128])
    wload("kv", w_kv_d)
    wload("ba", w_ba_d)
    off_pw, _ = WOFF["poolw"]

    def load_own_weights():
        wload("q", w_q_d)
        wload("zg", w_zg_d)
        wload("ga", w_ga_d)
        wload("pl", w_pl_d)
        wload("out", w_out_d)
        DMA("sp", yas[:, :], poolw_d)
        DMA("sp", tmpw[:, :], tabd(T_PS))
        TT("pool", yas[:, :], yas[:, :], tmpw[:, :], ALU.mult)
        CP("pool", V(WB["poolw"], WBIG.h[:, off_pw:off_pw + 1024]), yas[:, :])
    ACT(negA[:, :], alog_t, AF.Exp)
    TS("dve", negA[:, :], negA[:, :], -1.0, None, ALU.mult)

    xt = sb("xt", [128, D])
    xn = sb("xn", [128, D], BF16)
    hT = sb("hT", [128, 8, 128], BF16)
    bst = sb("bst", [128, 2, 6])
    mv = sb("mv", [128, 8])
    gm = sb("gm", [128, 16])
    NCH = 24
    pre = [sb("pre%d" % c, [128, 131]) for c in range(NCH)]
    for c in range(NCH):
        MSET("pool", pre[c][:, 0:3], 0.0)
    cacc = [sb("cacc%d" % i, [128, 128]) for i in range(2)]
    ksil = [sb("ksil%d" % i, [128, 128]) for i in range(2)]
    ksq = [sb("ksq%d" % i, [128, 128], BF16) for i in range(2)]
    rnb = [sb("rnb%d" % i, [128, 128]) for i in range(2)]
    khatT = [sb("khatT%d" % h, [128, 128], BF16) for h in range(2)]
    qhatT = [sb("qhatT%d" % h, [128, 128], BF16) for h in range(2)]
    vTb = [sb("vT%d" % i, [128, 128], BF16) for i in range(2)]
    bv = [sb("bv%d" % h, [128, 128], BF16) for h in range(2)]
    kbg = [sb("kbg%d" % h, [128, 128], BF16) for h in range(2)]
    kgA = [sb("kgA%d" % h, [128, 128], BF16) for h in range(2)]
    kgB = [sb("kgB%d" % h, [128, 128], BF16) for h in range(2)]
    sm = sb("sm", [128, 160])
    S32 = [sb("S32_%d" % h, [128, 128]) for h in range(NH)]
    Sbf = [sb("Sbf_%d" % h, [128, 128], BF16) for h in range(NH)]
    for h in range(NH):
        MSET("pool", S32[h][:, :], 0.0)
        MSET("pool", Sbf[h][:, :], 0.0)
    Amat = [sb("Amat%d" % i, [128, 128]) for i in range(2)]
    Grep = [sb("Grep%d" % i, [128, 128]) for i in range(2)]
    Dex = [sb("Dex%d" % i, [128, 128]) for i in range(2)]
    Dl = [sb("Dl%d" % i, [128, 128]) for i in range(2)]
    DTe = [sb("DTe%d" % i, [128, 128]) for i in range(2)]
    Ebc = [sb("Ebc%d" % i, [128, 128]) for i in range(2)]
    NPB = 12
    pb = [[sb("pb%d_%d" % (l, i), [128, 128], BF16) for i in range(NPB)] for l in range(2)]
    pbi = [0, 0]

    def newpb_l(l):
        t = pb[l][pbi[l] % NPB]
        pbi[l] += 1
        return t
    u32 = [sb("u32_%d" % i, [128, 128]) for i in range(2)]
    wTb = [sb("wTb%d" % i, [128, 128], BF16) for i in range(2)]
    vnew = [sb("vnew%d" % i, [128, 128], BF16) for i in range(2)]
    attnT = [sb("attnT%d" % i, [128, 128], BF16) for i in range(2)]
    qgTa = [sb("qgTa%d" % i, [128, 128], BF16) for i in range(2)]
    qgTb = [sb("qgTb%d" % i, [128, 128], BF16) for i in range(2)]
    for i in range(2):
        MSET("pool", vnew[i][:, :], 0.0)
        MSET("pool", qgTa[i][:, :], 0.0)
        MSET("pool", qgTb[i][:, :], 0.0)
    rs = sb("rs", [128, 16])
    junk = sb("junk", [128, 128])
    mixb = sb("mixb", [128, D], BF16)
    mixT = sb("mixT", [128, 8, 128], BF16)
    pbuf = [sb("pbuf%d" % g, [128, 143]) for g in range(4)]
    ptmp = [sb("ptmp%d" % i, [128, 143]) for i in range(2)]
    for g in range(4):
        MSET("pool", pbuf[g][:, 0:15], 0.0)
    dTp = [sb("dTp%d" % g, [128, 128], BF16) for g in range(4)]

    def lnorm(src, eps):
        for c in range(2):
            S.op("dve", lambda e, c=c: e.bn_stats(out=bst.h[:, c, :], in_=src.h[:, c * 512:(c + 1) * 512]),
                 reads=[src.b], writes=[bst.b])
        S.op("dve", lambda e: e.bn_aggr(out=mv.h[:, 0:2], in_=bst.h[:, :, :]), reads=[bst.b], writes=[mv.b])
        if "ln" in FL:
            RSQ(mv[:, 3:4], mv[:, 1:2], 1.0, eps)
        else:
            ACT(mv[:, 2:3], mv[:, 1:2], AF.Sqrt, bias=eps)
            RECIP(mv[:, 3:4], mv[:, 2:3])
        STT("dve", mv[:, 4:5], mv[:, 0:1], -1.0, mv[:, 3:4], ALU.mult, ALU.mult)
        return mv[:, 3:4], mv[:, 4:5]

    def affine(dst, ig, ib):
        DMA("sp", tmpw[:, :], tabd(ig))
        TT("pool", dst[:, :], dst[:, :], tmpw[:, :], ALU.mult)
        DMA("sp", tmpw[:, :], tabd(ib))
        TT("pool", dst[:, :], dst[:, :], tmpw[:, :], ALU.add)

    NSTP = NPRE // 4
    scrA = [WOFF["q"][0], WOFF["ba"][0]]
    scrB = [WOFF["pl"][0], WOFF["poolw"][0] + 1024]

    def salloc(shape, dt, name, reg=None):
        reg = scrA if reg is None else reg
        n = 1
        for s_ in shape[1:]:
            n *= s_
        ne = n * (2 if dt == F32 else 1)
        if reg[0] % 2:
            reg[0] += 1
        t = wview(reg[0], shape, dt, name)
        reg[0] += ne
        assert reg[0] <= reg[1], (name, reg)
        return t

    hT4 = [salloc([128, 8, 512], BF16, "hT4_%d" % i, scrB) for i in range(2)]
    smx2 = [salloc([128, 16 * 32], F32, "smx%d" % i) for i in range(2)]
    hist = salloc([128, 16, 3], F32, "hist")
    workb = [salloc([128, 515], F32, "work%d" % i) for i in range(2)]
    caccb = [salloc([128, 512], F32, "caccb%d" % i) for i in range(2)]
    rnb4 = salloc([128, 512], F32, "rnb4")
    ksq4 = salloc([128, 512], BF16, "ksq4")
    khT4 = [salloc([128, 512], BF16, "khT4_%d" % i) for i in range(2)]
    vT4 = salloc([128, 512], BF16, "vT4")
    bv4 = [salloc([128, 4, 128], BF16, "bv4_%d" % i) for i in range(2)]
    kbg4 = [salloc([128, 4, 128], BF16, "kbg4_%d" % i) for i in range(2)]
    kgA4 = [salloc([128, 4, 128], BF16, "kgA4_%d" % i) for i in range(3)]
    kgB4 = [salloc([128, 4, 128], BF16, "kgB4_%d" % i) for i in range(3)]
    Am4 = salloc([128, 4, 128], F32, "Am4")
    Dex4 = salloc([128, 4, 128], F32, "Dex4")
    NPC = 10
    pch = [salloc([128, 4, 128], BF16, "pch%d" % i) for i in range(NPC)]
    pci = [0]

    def newpc():
        t = pch[pci[0] % NPC]
        pci[0] += 1
        return t
    u4 = [salloc([128, 4, 128], F32, "u4_%d" % i) for i in range(2)]
    wT4 = [salloc([128, 4, 128], BF16, "wT4_%d" % i) for i in range(2)]
    vn4 = [salloc([128, 128], BF16, "vn4_%d" % i) for i in range(2)]
    if NSTP > 0:
        MSET("pool", hist[:, :, :], 0.0)
        for i in range(2):
            MSET("pool", vn4[i][:, :], 0.0)

    F1R = RR([fbanks[0], fbanks[1]])
    F2R = RR([fbanks[2], fbanks[3]])
    RC = RR(pobanks)
    PT1 = RR([ptbanks[0]])
    PT2 = RR([ptbanks[1]])

    def bank3(bk, n=4, w=128):
        return V(bk[0].b, bk[0].h[:, 0:n * w].rearrange("p (s c) -> p s c", s=n))

    def bankw(bk):
        return V(bk[0].b, bk[0].h[:, 0:512])

    def bc_last(v2, n):
        return V(v2.b, v2.ap.unsqueeze(2).to_broadcast([128, v2.ap.shape[1], n]))

    def bc_mid(v2, s_):
        return V(v2.b, v2.ap.unsqueeze(1).to_broadcast([128, s_, v2.ap.shape[1]]))

    def smq(T, q):
        return smx2[T % 2][:, q * 32:(q + 1) * 32]

    def smq3(T, q):
        sx = smx2[T % 2]
        return V(sx.b, sx.h[:, q * 32:(q + 1) * 32].rearrange("p (s h) -> p s h", h=8))

    def smh(T, q, h):
        sx = smx2[T % 2]
        return V(sx.b, sx.h[:, q * 32:(q + 1) * 32].rearrange("p (s h) -> p h s", h=8)[:, h, :])

    def smc(T, q, s_, h):
        return smx2[T % 2][:, q * 32 + s_ * 8 + h:q * 32 + s_ * 8 + h + 1]
    (Q_BETA, Q_NBETA, Q_G, Q_EGC, Q_BGE, Q_EKGA, Q_EKGB, Q_EGLA, Q_EGLB, Q_T1, Q_T2) = range(11)

    def pre_f1(g):
        T, h = divmod(g, 8)
        hTT = hT4[T % 2]
        for kind in ("k", "v"):
            c = (0 if kind == "k" else 8) + h
            ki = 0 if kind == "k" else 1
            pp = F1R.get_bank()
            for kc in range(8):
                MM(bankw(pp), wv("kv", kc, c * 128, c * 128 + 128), hTT[:, kc, :],
                   start=(kc == 0), stop=(kc == 7))
            wk = workb[ki]
            CP("pool", wk[:, 0:3], hist[:, c, :])
            CP("act", wk[:, 3:515], bankw(pp))
            ca = caccb[ki]
            cw = [smalls[:, o_convw + c * 4 + k:o_convw + c * 4 + k + 1] for k in range(4)]
            TS("dve", ca[:, :], wk[:, 3:515], cw[3], None, ALU.mult)
            STT("dve", ca[:, :], wk[:, 2:514], cw[2], ca[:, :], ALU.mult, ALU.add)
            STT("dve", ca[:, :], wk[:, 1:513], cw[1], ca[:, :], ALU.mult, ALU.add)
            STT("dve", ca[:, :], wk[:, 0:512], cw[0], ca[:, :], ALU.mult, ALU.add)
            CP("pool", hist[:, c, :], wk[:, 512:515])
            if "psilu" in FL:
                SIGM(rnb4[:, :], ca[:, :])
            else:
                ACT(rnb4[:, :], ca[:, :], AF.Exp, scale=-1.0)
                TS("pool", rnb4[:, :], rnb4[:, :], 1.0, None, ALU.add)
                RECIP(rnb4[:, :], rnb4[:, :])
            if kind == "v":
                TT("pool", vT4[:, :], ca[:, :], rnb4[:, :], ALU.mult)
                ptb_ = PT1.get_bank()
                for s_ in range(4):
                    TR(ptb_[s_].v(), vT4[:, s_ * 128:(s_ + 1) * 128])
                TT("dve", bv4[g % 2][:, :, :], bank3(ptb_), bc_last(smh(T, Q_BETA, h), 128), ALU.mult)
            else:
                TT("pool", ca[:, :], ca[:, :], rnb4[:, :], ALU.mult)
                ACT(ksq4[:, :], ca[:, :], AF.Square)
                pss = F1R.get_bank()
                MM(bankw(pss), onesb[:, :], ksq4[:, :])
                ACT(rnb4[:, :], bankw(pss), AF.Ln, bias=L2_EPS)
                ACT(rnb4[:, :], rnb4[:, :], AF.Exp, scale=-0.5)
                TT("pool", khT4[g % 2][:, :], ca[:, :], rnb4[:, :], ALU.mult)
                ptb_ = PT1.get_bank()
                for s_ in range(4):
                    TR(ptb_[s_].v(), khT4[g % 2][:, s_ * 128:(s_ + 1) * 128])
                TT("dve", kbg4[g % 2][:, :, :], bank3(ptb_), bc_last(smh(T, Q_BGE, h), 128), ALU.mult)
                TT("dve", kgA4[g % 3][:, :, :], bank3(ptb_), bc_last(smh(T, Q_EKGA, h), 128), ALU.mult)
                TT("dve", kgB4[g % 3][:, :, :], bank3(ptb_), bc_last(smh(T, Q_EKGB, h), 128), ALU.mult)

    def pre_f2(g):
        T, h = divmod(g, 8)
        st = g % 2
        for s_ in range(4):
            ACT(Am4[:, s_, :], cst(C_UALL), AF.Identity, bias=zcol, scale=smc(T, Q_G, s_, h))
        pd = F2R.get_bank()
        for s_ in range(4):
            MM(pd[s_].v(), Am4[:, s_, :], cst(C_BST))
        ACT(Dex4[:, :, :], bank3(pd), AF.Exp)
        TT("pool", Dex4[:, :, :], Dex4[:, :, :], bc_mid(cst(C_MLS), 4), ALU.mult)
        TT("pool", Dex4[:, :, :], Dex4[:, :, :], bc_last(smh(T, Q_NBETA, h), 128), ALU.mult)
        pg = F2R.get_bank()
        for s_ in range(4):
            ks_ = khT4[st][:, s_ * 128:(s_ + 1) * 128]
            MM(pg[s_].v(), ks_, ks_)
        P = newpc()
        TT("dve", P[:, :, :], bank3(pg), Dex4[:, :, :], ALU.mult)
        ptb_ = PT2.get_bank()
        for s_ in range(4):
            TR(ptb_[s_].v(), P[:, s_, :])
        PTt = newpc()
        CP("act", PTt[:, :, :], bank3(ptb_))
        TTk = newpc()
        TT("pool", TTk[:, :, :], PTt[:, :, :], bc_mid(identb[:, :], 4), ALU.add)
        for lev in range(5):
            pn = F2R.get_bank()
            for s_ in range(4):
                MM(pn[s_].v(), PTt[:, s_, :], P[:, s_, :])
            Pn = newpc()
            CP("act", Pn[:, :, :], bank3(pn))
            if lev < 4:
                pnt = F2R.get_bank()
                for s_ in range(4):
                    MM(pnt[s_].v(), P[:, s_, :], PTt[:, s_, :])
                PnT = newpc()
                CP("act", PnT[:, :, :], bank3(pnt))
            px = F2R.get_bank()
            for s_ in range(4):
                MM(px[s_].v(), Pn[:, s_, :], TTk[:, s_, :])
            TTn = newpc()
            TT("dve", TTn[:, :, :], bank3(px), TTk[:, :, :], ALU.add)
            P, TTk = Pn, TTn
            if lev < 4:
                PTt = PnT
        pu = F2R.get_bank()
        for s_ in range(4):
            MM(pu[s_].v(), TTk[:, s_, :], bv4[st][:, s_, :])
        CP("act", u4[st][:, :, :], bank3(pu))
        pwt = F2R.get_bank()
        for s_ in range(4):
            MM(pwt[s_].v(), kbg4[st][:, s_, :], TTk[:, s_, :])
        CP("act", wT4[st][:, :, :], bank3(pwt))

    def pre_recur(g):
        T, h = divmod(g, 8)
        st = g % 2
        vn = vn4[st]
        for s_ in range(4):
            for (lo, hi, kg, qe) in ((0, 64, kgA4, Q_EGLA), (64, 128, kgB4, Q_EGLB)):
                pws = RC.get_bank()
                MM(pws[0].v(), wT4[st][:, s_, :], Sbf[h][:, :])
                TT("dve", vn[lo:hi, :], u4[st][lo:hi, s_, :], pws[0].v(r0=lo, r1=hi), ALU.subtract)
                psu = RC.get_bank()
                MM(psu[0].v(), kg[g % 3][:, s_, :], vn[:, :])
                STT("dve", Sbf[h][:, :], S32[h][:, :], smc(T, qe, s_, h), psu[0].v(), ALU.mult, ALU.add)
                STT("dve", S32[h][:, :], S32[h][:, :], smc(T, qe, s_, h), psu[0].v(), ALU.mult, ALU.add)

    def pre_amble(T):
        hTT = hT4[T % 2]
        for s_ in range(4):
            n = 4 * T + s_
            DMA("sp", xt[:, :], xp[n * 128:(n + 1) * 128, :])
            rstd, nb = lnorm(xt, LN_EPS)
            ACT(xn[:, :], xt[:, :], AF.Identity, bias=nb, scale=rstd)
            mcol = smalls[:, o_mask + n:o_mask + n + 1]
            TS("dve", gm[:, 0:8], lng_c, mcol, None, ALU.mult)
            TS("dve", gm[:, 8:16], lnb_c, mcol, None, ALU.mult)
            S.begin_atomic()
            ptg = PT1.get_bank()
            for kc in range(8):
                TR(ptg[kc].v(), xn[:, kc * 128:(kc + 1) * 128])
            for kc in range(8):
                TS("dve", hTT[:, kc, s_ * 128:(s_ + 1) * 128], ptg[kc].v(), gm[:, kc:kc + 1],
                   gm[:, 8 + kc:9 + kc], ALU.mult, ALU.add)
            S.end_atomic()
        S.begin_atomic()
        pba = F1R.get_bank()
        for s_ in range(4):
            for kc in range(8):
                MM(V(pba[0].b, pba[0].h[:, s_ * 16:s_ * 16 + 16]), hTT[:, kc, s_ * 128:(s_ + 1) * 128],
                   wv("ba", kc, 0, 16), start=(kc == 0), stop=(kc == 7))
        pba3 = pba[0].h[:, 0:64].rearrange("p (s c) -> p s c", s=4)
        braw = V(pba[0].b, pba3[:, :, 0:8])
        araw = V(pba[0].b, pba3[:, :, 8:16])
        ACT(smq3(T, Q_T1), braw, AF.Exp, scale=-1.0)
        TT("dve", smq3(T, Q_T2), araw, bc_mid(dtb_t, 4), ALU.add)
        S.end_atomic()
        TS("dve", smq(T, Q_T1), smq(T, Q_T1), 1.0, None, ALU.add)
        RECIP(smq(T, Q_BETA), smq(T, Q_T1))
        TS("dve", smq(T, Q_NBETA), smq(T, Q_BETA), -1.0, None, ALU.mult)
        ACT(smq(T, Q_T2), smq(T, Q_T2), AF.Exp)
        ACT(smq(T, Q_T2), smq(T, Q_T2), AF.Ln, bias=1.0)
        TT("dve", smq3(T, Q_G), smq3(T, Q_T2), bc_mid(negA[:, :], 4), ALU.mult)
        S.begin_atomic()
        pgc = F1R.get_bank()
        for i, ci in enumerate((C_UBLK, C_BLK, C_MA, C_MB)):
            MM(V(pgc[0].b, pgc[0].h[:, i * 32:i * 32 + 32]), cst(ci), smq(T, Q_G))

        def pg_(i):
            return V(pgc[0].b, pgc[0].h[:, i * 32:i * 32 + 32])
        ACT(smq(T, Q_EGC), pg_(0), AF.Exp)
        CP("act", smq(T, Q_T1), pg_(1))
        ACT(smq(T, Q_EGLA), pg_(2), AF.Exp)
        ACT(smq(T, Q_EGLB), pg_(3), AF.Exp)
        TT("dve", smq(T, Q_T2), smq(T, Q_T1), pg_(0), ALU.subtract)
        S.end_atomic()
        TT("dve", smq(T, Q_BGE), smq(T, Q_EGC), smq(T, Q_BETA), ALU.mult)
        ACT(smq(T, Q_T2), smq(T, Q_T2), AF.Exp)
        TS("dve", smq(T, Q_EKGA), smq(T, Q_T2), pmA, None, ALU.mult)
        TS("dve", smq(T, Q_EKGB), smq(T, Q_T2), pmB, None, ALU.mult)

    def cap(fn, *args):
        S.capture = []
        fn(*args)
        l = S.capture
        S.capture = None
        return l

    GT = NSTP * 8
    if NSTP > 0:
        S.replay([cap(pre_amble, 0)])
    nextpre, ppos, pstep = [], 0, 0
    for g in range(GT + 2 if NSTP > 0 else 0):
        lists = []
        if 0 <= g - 2 < GT:
            lists.append(cap(pre_recur, g - 2))
        if 0 <= g - 1 < GT:
            lists.append(cap(pre_f2, g - 1))
        if g < GT:
            lists.append(cap(pre_f1, g))
            T, h = divmod(g, 8)
            if T + 1 < NSTP:
                if h == 2:
                    nextpre = cap(pre_amble, T + 1)
                    ppos = 0
                    pstep = (len(nextpre) + 4) // 5
                if 2 <= h < 7:
                    pend = min(ppos + pstep, len(nextpre))
                    while 0 < pend < len(nextpre) and nextpre[pend - 1][6]:
                        pend += 1
                    lists.append(nextpre[ppos:pend])
                    ppos = pend
        S.replay(lists, chunk=1)

    if NSTP > 0:
        for c in range(16):
            CP("pool", pre[c][:, 0:3], hist[:, c, :])
        CP("pool", hT[:, :, :], hT4[(NSTP - 1) % 2][:, :, 384:512])
    S.barrier()
    load_own_weights()

    for n in range(NPRE - 1 if NPRE > 0 else 0, NT):
        own = n >= NPRE
        qact = True
        hist_only = not own
        io = n - NPRE
        if not hist_only:
            DMA("sp", xt[:, :], xp[n * 128:(n + 1) * 128, :])
            rstd, nb = lnorm(xt, LN_EPS)
            ACT(xn[:, :], xt[:, :], AF.Identity, bias=nb, scale=rstd)
            mcol = smalls[:, o_mask + n:o_mask + n + 1]
            TS("dve", gm[:, 0:8], lng_c, mcol, None, ALU.mult)
            TS("dve", gm[:, 8:16], lnb_c, mcol, None, ALU.mult)
            ptg = PT.group(8)
            for kc in range(8):
                TR(ptg[kc].v(), xn[:, kc * 128:(kc + 1) * 128])
            for kc in range(8):
                TS("dve", hT[:, kc, :], ptg[kc].v(), gm[:, kc:kc + 1], gm[:, 8 + kc:9 + kc], ALU.mult, ALU.add)

            pba = PQ.get()
            for kc in range(8):
                MM(pba.v(0, 16), hT[:, kc, :], wv("ba", kc, 0, 16), start=(kc == 0), stop=(kc == 7))
            ACT(sm[:, 24:32], pba.v(0, 8), AF.Exp, scale=-1.0)
            TS("dve", sm[:, 24:32], sm[:, 24:32], 1.0, None, ALU.add)
            RECIP(sm[:, 0:8], sm[:, 24:32])
            TS("dve", sm[:, 8:16], sm[:, 0:8], -1.0, None, ALU.mult)
            TT("dve", sm[:, 80:88], pba.v(8, 16), dtb_t, ALU.add)
            ACT(sm[:, 80:88], sm[:, 80:88], AF.Exp)
            ACT(sm[:, 80:88], sm[:, 80:88], AF.Ln, bias=1.0)
            TT("dve", sm[:, 16:24], sm[:, 80:88], negA[:, :], ALU.mult)
            pgc = PQ.get()
            for i, ci in enumerate((C_UBLK, C_BLK, C_MA, C_MB)):
                MM(pgc.v(i * 8, i * 8 + 8), cst(ci), sm[:, 16:24])
            ACT(sm[:, 32:40], pgc.v(0, 8), AF.Exp)
            TT("dve", sm[:, 40:48], sm[:, 32:40], sm[:, 0:8], ALU.mult)
            CP("act", sm[:, 88:96], pgc.v(8, 16))
            TT("dve", sm[:, 80:88], sm[:, 88:96], pgc.v(0, 8), ALU.subtract)
            ACT(sm[:, 80:88], sm[:, 80:88], AF.Exp)
            TS("dve", sm[:, 48:56], sm[:, 80:88], pmA, None, ALU.mult)
            TS("dve", sm[:, 56:64], sm[:, 80:88], pmB, None, ALU.mult)
            ACT(sm[:, 64:80], pgc.v(16, 32), AF.Exp)

        if qact:
            for g in range(4):
                pp = PQ.get()
                for kc in range(8):
                    MM(pp.v(), wv("pl", kc, g * 128, g * 128 + 128), hT[:, kc, :], start=(kc == 0), stop=(kc == 7))
                CP("act", pbuf[g][:, 15:143], pp.v())
        if own:
            for half in range(2):
                pw = PW.get()
                for kc in range(8):
                    MM(pw.v(), hT[:, kc, :], wv("ga", kc, half * 512, half * 512 + 512),
                       start=(kc == 0), stop=(kc == 7))
                if "gate" in FL:
                    SIGM(tmpw[:, half * 512:(half + 1) * 512], pw.v())
                else:
                    ACT(tmpw[:, half * 512:(half + 1) * 512], pw.v(), AF.Sigmoid)
            pya = [PW.get(), PW.get()]
            for g in range(4):
                w = 2 << g
                src = pbuf[g]
                lo = 1
                sh = 1
                cur = src
                k = 0
                while sh < w:
                    dstb = ptmp[k % 2]
                    TT("pool", dstb[:, lo:143], cur[:, lo:143], cur[:, lo - sh:143 - sh], ALU.add)
                    cur = dstb
                    sh *= 2
                    lo = 2 * sh - 1
                    k += 1
                STT("dve", dTp[g][:, :], cur[:, 15:143], 1.0 / w, src[:, 15:143], ALU.mult, ALU.subtract)
                if io == 0:
                    TT("pool", junk[:, 0:16], cur[:, 15:31], smalls[:, o_invc + g * 16:o_invc + g * 16 + 16], ALU.mult)
                    TT("pool", dTp[g][:, 0:16], junk[:, 0:16], src[:, 15:31], ALU.subtract)
                MM(pya[g // 2].v((g % 2) * 256, (g % 2) * 256 + 256), dTp[g][:, :],
                   V(WB["poolw"], WBIG.h[:, off_pw + g * 256:off_pw + g * 256 + 256]))
            for half in range(2):
                TT("dve", yas[:, half * 512:(half + 1) * 512], pya[half].v(), tmpw[:, half * 512:(half + 1) * 512], ALU.mult)
            for half in range(2):
                pw = PW.get()
                for kc in range(8):
                    MM(pw.v(), hT[:, kc, :], wv("zg", kc, half * 512, half * 512 + 512),
                       start=(kc == 0), stop=(kc == 7))
                ACT(tmpw[:, half * 512:(half + 1) * 512], pw.v(), AF.Silu)
        if qact:
            for g in range(4):
                CP("pool", pbuf[g][:, 0:15], pbuf[g][:, 128:143])

        def head_body(h, own=own, qact=qact, io=io, hist_only=hist_only):
            i2 = h % 2
            PQ = PQL[i2]
            PT = PTL[i2]
            newpb = lambda: newpb_l(i2)
            kinds = ("q",) if hist_only else ("k", "v", "q")
            for kind in kinds:
                c = {"k": 0, "v": 8, "q": 16}[kind] + h
                pp = PQ.get()
                for kc in range(8):
                    if kind == "q":
                        rhsw = wv("q", kc, h * 128, h * 128 + 128)
                    else:
                        rhsw = wv("kv", kc, c * 128, c * 128 + 128)
                    MM(pp.v(), rhsw, hT[:, kc, :], start=(kc == 0), stop=(kc == 7))
                prc = pre[c]
                CP("act", prc[:, 3:131], pp.v())
                if hist_only:
                    CP("pool", prc[:, 0:3], prc[:, 128:131])
                    continue
                ca = cacc[c % 2]
                cw = [smalls[:, o_convw + c * 4 + k:o_convw + c * 4 + k + 1] for k in range(4)]
                TS("dve", ca[:, :], prc[:, 3:131], cw[3], None, ALU.mult)
                STT("dve", ca[:, :], prc[:, 2:130], cw[2], ca[:, :], ALU.mult, ALU.add)
                STT("dve", ca[:, :], prc[:, 1:129], cw[1], ca[:, :], ALU.mult, ALU.add)
                STT("dve", ca[:, :], prc[:, 0:128], cw[0], ca[:, :], ALU.mult, ALU.add)
                if SUB >= 3:
                    CP("pool", prc[:, 0:3], prc[:, 128:131])
                if SUB < 4:
                    continue
                if kind == "v":
                    vt = vTb[i2]
                    if "osilu" in FL:
                        SIGM(rnb[c % 2][:, :], ca[:, :])
                        TT("pool", vt[:, :], ca[:, :], rnb[c % 2][:, :], ALU.mult)
                    else:
                        ACT(vt[:, :], ca[:, :], AF.Silu)
                    pt = PT.get()
                    TR(pt.v(), vt[:, :])
                    TS("dve", bv[i2][:, :], pt.v(), sm[:, h:h + 1], None, ALU.mult)
                elif SUB >= 5:
                    ks = ksil[c % 2]
                    if "osilu" in FL:
                        SIGM(rnb[c % 2][:, :], ca[:, :])
                        TT("pool", ks[:, :], ca[:, :], rnb[c % 2][:, :], ALU.mult)
                    else:
                        ACT(ks[:, :], ca[:, :], AF.Silu)
                    kq = ksq[c % 2]
                    ACT(kq[:, :], ks[:, :], AF.Square)
                    pss = PQ.get()
                    MM(pss.v(), onesb[:, :], kq[:, :])
                    rn = rnb[c % 2]
                    if kind == "k":
                        if "orsq" in FL:
                            RSQ(rn[:, :], pss.v(), 1.0, L2_EPS)
                        else:
                            ACT(rn[:, :], pss.v(), AF.Sqrt, bias=L2_EPS)
                            RECIP(rn[:, :], rn[:, :])
                    else:
                        if "orsq" in FL:
                            RSQ(rn[:, :], pss.v(), 128.0, 128.0 * L2_EPS)
                        else:
                            ACT(rn[:, :], pss.v(), AF.Sqrt, bias=128.0 * L2_EPS, scale=128.0)
                            RECIP(rn[:, :], rn[:, :])
                    dst = khatT[i2] if kind == "k" else qhatT[i2]
                    TT("pool", dst[:, :], ks[:, :], rn[:, :], ALU.mult)
                    if kind == "k":
                        pt = PT.get()
                        TR(pt.v(), dst[:, :])
                        TS("dve", kbg[i2][:, :], pt.v(), sm[:, 40 + h:41 + h], None, ALU.mult)
                        ACT(kgA[i2][:, :], pt.v(), AF.Identity, bias=zcol, scale=sm[:, 48 + h:49 + h])
                        ACT(kgB[i2][:, :], pt.v(), AF.Identity, bias=zcol, scale=sm[:, 56 + h:57 + h])

            if hist_only:
                return
            A = Amat[i2]
            ACT(A[:, :], cst(C_UALL), AF.Identity, bias=zcol, scale=sm[:, 16 + h:17 + h])
            pd = PQ.get()
            MM(pd.v(), A[:, :], cst(C_BST))
            ACT(Dex[i2][:, :], pd.v(), AF.Exp)
            TT("pool", Dl[i2][:, :], Dex[i2][:, :], cst(C_MLS), ALU.mult)
            pg = PQ.get()
            MM(pg.v(), khatT[i2][:, :], khatT[i2][:, :])
            P = newpb()
            STT("dve", P[:, :], pg.v(), sm[:, 8 + h:9 + h], Dl[i2][:, :], ALU.mult, ALU.mult)
            pt = PT.get()
            TR(pt.v(), P[:, :])
            PTt = newpb()
            CP("act", PTt[:, :], pt.v())
            TTk = newpb()
            TT("pool", TTk[:, :], identb[:, :], PTt[:, :], ALU.add)
            for lev in range(5):
                pn = PQ.get()
                MM(pn.v(), PTt[:, :], P[:, :])
                Pn = newpb()
                CP("act", Pn[:, :], pn.v())
                if lev < 4:
                    pnt = PQ.get()
                    MM(pnt.v(), P[:, :], PTt[:, :])
                    PnT = newpb()
                    CP("act", PnT[:, :], pnt.v())
                px = PQ.get()
                MM(px.v(), Pn[:, :], TTk[:, :])
                TTn = newpb()
                TT("dve", TTn[:, :], px.v(), TTk[:, :], ALU.add)
                P, TTk = Pn, TTn
                if lev < 4:
                    PTt = PnT
            pu = PQ.get()
            MM(pu.v(), TTk[:, :], bv[i2][:, :])
            u = u32[i2]
            CP("act", u[:, :], pu.v())
            pwt = PQ.get()
            MM(pwt.v(), kbg[i2][:, :], TTk[:, :])
            wT = wTb[i2]
            CP("act", wT[:, :], pwt.v())
            po = None
            if own:
                pdt = PQ.get()
                MM(pdt.v(), cst(C_BST), A[:, :])
                ACT(DTe[i2][:, :], pdt.v(), AF.Exp)
                TT("pool", DTe[i2][:, :], DTe[i2][:, :], cst(C_MUI), ALU.mult)
                pa = PQ.get()
                MM(pa.v(), khatT[i2][:, :], qhatT[i2][:, :])
                TT("dve", attnT[i2][:, :], pa.v(), DTe[i2][:, :], ALU.mult)
                ACT(Grep[i2][:, :], cst(C_ONE), AF.Identity, bias=zcol, scale=sm[:, 16 + h:17 + h])
                pe_ = PQ.get()
                MM(pe_.v(), Grep[i2][:, :], cst(C_UBLK))
                ACT(Ebc[i2][:, :], pe_.v(), AF.Exp)
                TT("pool", qgTa[i2][:, 0:64], qhatT[i2][:, 0:64], Ebc[i2][:, 0:64], ALU.mult)
                TT("pool", qgTb[i2][:, 64:128], qhatT[i2][:, 64:128], Ebc[i2][:, 64:128], ALU.mult)
                po = POL[i2].get()
                MM(po.v(), qgTa[i2][:, :], Sbf[h][:, :], start=True, stop=False, hold=False)
            vn = vnew[i2]
            pws = PQ.get()
            MM(pws.v(), wT[:, :], Sbf[h][:, :])
            TT("dve", vn[0:64, :], u[0:64, :], pws.v(r0=0, r1=64), ALU.subtract)
            psu = PQ.get()
            MM(psu.v(), kgA[i2][:, :], vn[:, :])
            STT("dve", Sbf[h][:, :], S32[h][:, :], sm[:, 64 + h:65 + h], psu.v(), ALU.mult, ALU.add)
            STT("dve", S32[h][:, :], S32[h][:, :], sm[:, 64 + h:65 + h], psu.v(), ALU.mult, ALU.add)
            if own:
                MM(po.v(), qgTb[i2][:, :], Sbf[h][:, :], start=False, stop=False, hold=False)
            pws2 = PQ.get()
            MM(pws2.v(), wT[:, :], Sbf[h][:, :])
            TT("dve", vn[64:128, :], u[64:128, :], pws2.v(r0=64, r1=128), ALU.subtract)
            psu2 = PQ.get()
            MM(psu2.v(), kgB[i2][:, :], vn[:, :])
            STT("dve", Sbf[h][:, :], S32[h][:, :], sm[:, 72 + h:73 + h], psu2.v(), ALU.mult, ALU.add)
            STT("dve", S32[h][:, :], S32[h][:, :], sm[:, 72 + h:73 + h], psu2.v(), ALU.mult, ALU.add)
            if own:
                MM(po.v(), attnT[i2][:, :], vn[:, :], start=False, stop=True)
                ACT(junk[:, :], po.v(), AF.Square, accum=rs[:, h:h + 1])
                if "orsq" in FL:
                    RSQ(rs[:, 8 + h:9 + h], rs[:, h:h + 1], 1.0 / 128.0, RMS_EPS)
                else:
                    ACT(rs[:, 8 + h:9 + h], rs[:, h:h + 1], AF.Sqrt, bias=RMS_EPS, scale=1.0 / 128.0)
                    RECIP(rs[:, 8 + h:9 + h], rs[:, 8 + h:9 + h])
                STT("dve", ybn[:, h * 128:(h + 1) * 128], po.v(), rs[:, 8 + h:9 + h], onw_t[:, :], ALU.mult, ALU.mult)

        for h0 in range(0, NH if LVL >= 5 else 0, 2):
            lists = []
            for h in (h0, h0 + 1):
                S.capture = []
                head_body(h)
                lists.append(S.capture)
                S.capture = None
            S.replay(lists, chunk=1)

        if own and LVL >= 7:
            for half in range(2):
                hs = slice(half * 512, (half + 1) * 512)
                TT("dve", ybn[:, hs], ybn[:, hs], tmpw[:, hs], ALU.mult)
                pw = PW.get()
                for kc in range(8):
                    MM(pw.v(), hT[:, kc, :], wv("zg", kc, 1024 + half * 512, 1024 + half * 512 + 512),
                       start=(kc == 0), stop=(kc == 7))
                ACT(tmpw[:, hs], pw.v(), AF.Sigmoid)
                TT("dve", ybn[:, hs], ybn[:, hs], tmpw[:, hs], ALU.mult)
            TT("dve", mixb[:, :], yas[:, :], ybn[:, :], ALU.add)
            DMA("sp", tmpw[:, :], tabd(T_G0))
            DMA("sp", ybn[:, :], tabd(T_B0))
            ptg = PT.group(8)
            for kc in range(8):
                TR(ptg[kc].v(), mixb[:, kc * 128:(kc + 1) * 128])
            for kc in range(8):
                CP("act" if kc % 2 else "dve", mixT[:, kc, :], ptg[kc].v())
            htok = yas
            ACT(htok[:, :], xt[:, :], AF.Identity, bias=nb, scale=rstd)
            TT("dve", htok[:, :], htok[:, :], tmpw[:, :], ALU.mult)
            TT("dve", htok[:, :], htok[:, :], ybn[:, :], ALU.add)
            DMA("sp", tmpw[:, :], tabd(T_G1))
            r1 = ybn
            for half in range(2):
                pw = PW.get()
                for kc in range(8):
                    MM(pw.v(), mixT[:, kc, :], wv("out", kc, half * 512, half * 512 + 512),
                       start=(kc == 0), stop=(kc == 7))
                STT("dve", r1[:, half * 512:(half + 1) * 512], htok[:, half * 512:(half + 1) * 512], ALPHA,
                    pw.v(), ALU.mult, ALU.add)
            rstd1, nb1 = lnorm(r1, LN_EPS)
            h1 = yas
            ACT(h1[:, :], r1[:, :], AF.Identity, bias=nb1, scale=rstd1)
            DMA("sp", ybn[:, :], tabd(T_B1))
            TT("dve", h1[:, :], h1[:, :], tmpw[:, :], ALU.mult)
            TT("dve", h1[:, :], h1[:, :], ybn[:, :], ALU.add)
            a0 = ybn
            TS("dve", a0[:, :], h1[:, :], ALPHA, None, ALU.mult)
            DMA("sp", acc0_d[io * 128:(io + 1) * 128, :], a0[:, :], key="d:acc0w")
            if debug:
                DMA("sp", dbg_d[io * 128:(io + 1) * 128, :], h1[:, :], key="d:dbgw")
            CP("act", mixb[:, :], h1[:, :])
            ptg = PT.group(8)
            for kc in range(8):
                TR(ptg[kc].v(), mixb[:, kc * 128:(kc + 1) * 128])
            for kc in range(8):
                CP("act" if kc % 2 else "dve", mixT[:, kc, :], ptg[kc].v())
            DMA("sp", V(Buf("h1Tw%d" % io), h1T_d[:, :, io * 128:(io + 1) * 128]), mixT[:, :, :], key="d:h1Tw")

    S.barrier()

    h1Tg = wview(0, [128, 8, 512], BF16, "h1Tg")
    wup = [wview(4096 + i * 4096, [128, 8, 512], BF16, "wup%d" % i) for i in range(2)]
    wdn = [wview(12288 + i * 4096, [128, 4, 1024], BF16, "wdn%d" % i) for i in range(2)]
    actT = [wview(20480 + i * 2048, [128, 4, 512], BF16, "actT%d" % i) for i in range(2)]
    Wg = wview(24576, [128, 8, 1024], BF16, "Wg")
    Wp = wview(32768, [128, 2, 1024], BF16, "Wp")
    acc = wview(34816, [128, 4, 1024], F32, "acc")
    rTt = wview(43008, [128, 8, 128], BF16, "rTt")
    pTt = wview(44032, [128, 2, 128], BF16, "pTt")
    pbf = wview(44288, [128, 256], BF16, "pbf")
    ln2t = wview(44544, [128, 2, 1024], F32, "ln2t")
    DMA("pool", Wg[:, :, :], w_g_d.rearrange("(k p) c -> p k c", p=128))
    DMA("pool", Wp[:, :, :], w_p_d.rearrange("(k p) c -> p k c", p=128))
    DMA("sp", ln2t[:, 0, :], tabd(T_G2))
    DMA("sp", ln2t[:, 1, :], tabd(T_B2))
    h1Tg2 = [h1Tg, wview(48640, [128, 8, 512], BF16, "h1Tg_b")]
    acc2 = [acc, wview(52736, [128, 4, 1024], F32, "acc_b")]
    ptl = wview(60928, [128, 256], F32, "ptl")
    relu_t = [ybn, yas]
    NFB = DFF // 512
    blkc = [0]
    PWF = RR([wbanks[0], wbanks[1]])
    PWP = RR([wbanks[2], wbanks[3]])

    class _TV:
        def __init__(self, h, b_):
            self.h, self.b = h, b_

        def __getitem__(self, idx):
            return V(self.b, self.h[idx])

    def ffn_core(G):
        hg = h1Tg2[G % 2]
        ac = acc2[G % 2]
        DMA("sp", hg[:, :, :], h1T_d[:, :, G * 512:(G + 1) * 512])
        DMA("sp", ac[:, :, :], acc0_d[G * 512:(G + 1) * 512, :].rearrange("(t p) d -> p t d", p=128))
        for fb in range(NFB):
            wu = wup[blkc[0] % 2]
            wd = wdn[blkc[0] % 2]
            at = actT[blkc[0] % 2]
            blkc[0] += 1
            DMA("pool", wu[:, :, :], w_up_d[:, fb * 512:(fb + 1) * 512].rearrange("(k p) c -> p k c", p=128))
            DMA("pool", wd[:, :, :], w_down_d[fb * 512:(fb + 1) * 512, :].rearrange("(k p) c -> p k c", p=128))
            for fc in range(4):
                S.begin_atomic()
                pw = PWF.get()
                for kc in range(8):
                    MM(pw.v(), wu[:, kc, fc * 128:(fc + 1) * 128], hg[:, kc, :], start=(kc == 0), stop=(kc == 7))
                rt = relu_t[fc % 2]
                ACT(rt[:, 0:512], pw.v(), AF.Relu)
                S.end_atomic()
                TT("pool", at[:, fc, :], rt[:, 0:512], rt[:, 0:512], ALU.mult)
            for tt in range(4):
                for half in range(2):
                    S.begin_atomic()
                    pw = PWF.get()
                    for fc in range(4):
                        MM(pw.v(), at[:, fc, tt * 128:(tt + 1) * 128], wd[:, fc, half * 512:(half + 1) * 512],
                           start=(fc == 0), stop=(fc == 3))
                    TT("dve", ac[:, tt, half * 512:(half + 1) * 512], ac[:, tt, half * 512:(half + 1) * 512],
                       pw.v(), ALU.add)
                    S.end_atomic()

    def ple_stage(G):
        ac = acc2[G % 2]
        for tt in range(4):
            io = G * 4 + tt
            CP("act", mixb[:, :], ac[:, tt, :])
            S.begin_atomic()
            ptg = PT.group(8)
            for kc in range(8):
                TR(ptg[kc].v(), mixb[:, kc * 128:(kc + 1) * 128])
            for kc in range(8):
                CP("act" if kc % 2 else "dve", rTt[:, kc, :], ptg[kc].v())
            S.end_atomic()
            DMA("sp", ptl[:, :], p_own[io * 128:(io + 1) * 128, :])
            CP("act", pbf[:, :], ptl[:, :])
            S.begin_atomic()
            ptg = PT.group(2)
            for c in range(2):
                TR(ptg[c].v(), pbf[:, c * 128:(c + 1) * 128])
            for c in range(2):
                CP("act", pTt[:, c, :], ptg[c].v())
            S.end_atomic()
            sg = tmpw
            for half in range(2):
                S.begin_atomic()
                pw = PWP.get()
                for kc in range(8):
                    MM(pw.v(), rTt[:, kc, :], Wg[:, kc, half * 512:(half + 1) * 512], start=(kc == 0), stop=(kc == 7))
                ACT(sg[:, half * 512:(half + 1) * 512], pw.v(), AF.Sigmoid)
                S.end_atomic()
                S.begin_atomic()
                pw2 = PWP.get()
                for c in range(2):
                    MM(pw2.v(), pTt[:, c, :], Wp[:, c, half * 512:(half + 1) * 512], start=(c == 0), stop=(c == 1))
                TT("dve", sg[:, half * 512:(half + 1) * 512], sg[:, half * 512:(half + 1) * 512], pw2.v(), ALU.mult)
                S.end_atomic()
            r2 = _TV(ac.h[:, tt, :], ac.b)
            TT("dve", r2[:, :], sg[:, :], r2[:, :], ALU.add)
            rstd2, nb2 = lnorm(r2, LN_EPS)
            h2 = tmpw
            ACT(h2[:, :], r2[:, :], AF.Identity, bias=nb2, scale=rstd2)
            TT("dve", h2[:, :], h2[:, :], ln2t[:, 0, :], ALU.mult)
            TT("dve", h2[:, :], h2[:, :], ln2t[:, 1, :], ALU.add)
            DMA("sp", V(Buf("outw%d" % io), out_d[io * 128:(io + 1) * 128, :]), h2[:, :], key="d:outw")

    def capf(fn, *args):
        S.capture = []
        fn(*args)
        l = S.capture
        S.capture = None
        return l

    NGe = NG if LVL >= 8 else 0
    if NGe > 0:
        ffn_core(0)
    for G in range(NGe):
        lists = [capf(ple_stage, G)]
        if G + 1 < NGe:
            lists.append(capf(ffn_core, G + 1))
        S.replay(lists, chunk=1)

    fin = [(k, v) for k, v in S.dma_cnt.items()]
    S.wait_all("sp", fin)
    S.emit()
    return nc, used[0], S.n_sems


def make_consts():
    idx = np.arange(128)
    m, i = idx[:, None], idx[None, :]
    same = (m // 64) == (i // 64)
    c = np.zeros((10, 128, 128), np.float32)
    c[0] = np.eye(128)
    c[1] = (m <= i) & same
    c[2] = same
    c[3] = (m < 64) & (i >= 0)
    c[4] = (m >= 64) & (i >= 0)
    c[5] = (m <= i)
    c[6] = (m > i)
    c[7] = (m > i) & same
    c[8] = (m <= i) & same
    c[9] = 1.0
    return np.ascontiguousarray(c.transpose(1, 0, 2).reshape(128, 10 * 128))


def core_inputs(inp, b, j, NPRE, NOWN, seq):
    NT = NPRE + NOWN
    x = np.asarray(inp["x"], np.float32)
    own0 = j * NOWN * 128
    start = own0 - NPRE * 128
    xp = np.zeros((NT * 128, D), np.float32)
    lo = max(start, 0)
    xp[lo - start:] = x[b, lo:own0 + NOWN * 128]
    mask = np.zeros(NT, np.float32)
    mask[(lo - start) // 128:] = 1.0
    w_in = np.asarray(inp["w_in"], np.float32)[0]
    P0 = 512
    w_pl = w_in[:, 0:512]
    w_q = w_in[:, P0:P0 + 1024]
    w_k = w_in[:, P0 + 1024:P0 + 2048]
    w_v = w_in[:, P0 + 2048:P0 + 3072]
    w_z = w_in[:, 3584:4608]
    w_beta = w_in[:, 4608:4616]
    w_a = w_in[:, 4616:4624]
    w_ga = w_in[:, 4624:5648]
    w_gb = w_in[:, 5648:6672]
    conv_w = np.asarray(inp["conv_w"], np.float32)[0]
    cw = conv_w.T
    cw = np.concatenate([cw[1024:2048], cw[2048:3072], cw[0:1024]], 0)
    cw = cw.reshape(24, 128, 4).transpose(1, 0, 2).reshape(128, 96)
    NSM = 8 + 8 + NT + 16 + 16 + 64 + 2 + 96
    sm = np.zeros((128, NSM), np.float32)
    o = 0
    sm[:, o:o + 8] = np.asarray(inp["ln_in_g"], np.float32).reshape(8, 128).T; o += 8
    sm[:, o:o + 8] = np.asarray(inp["ln_in_b"], np.float32).reshape(8, 128).T; o += 8
    sm[:, o:o + NT] = mask[None, :]; o += NT
    sm[:, o:o + 8] = np.asarray(inp["a_log"], np.float32)[0][None, :]
    sm[:, o + 8:o + 16] = np.asarray(inp["dt_bias"], np.float32)[0][None, :]; o += 16
    o += 16
    invc = np.zeros((4, 16), np.float32)
    for g, w in enumerate((2, 4, 8, 16)):
        t = own0 + np.arange(16)
        invc[g] = 1.0 / np.minimum(t + 1, w)
    sm[:, o:o + 64] = invc.reshape(1, 64); o += 64
    sm[:64, o] = 1.0
    sm[64:, o + 1] = 1.0; o += 2
    sm[:, o:o + 96] = cw; o += 96
    tabs = np.zeros((128, 7 * D + 128), np.float32)
    for i, nm in enumerate(("ln_in_g", "ln_in_b", "pool_scale", "ln1_g", "ln1_b", "ln2_g", "ln2_b")):
        tabs[:, i * D:(i + 1) * D] = np.asarray(inp[nm], np.float32).reshape(-1)[None, :]
    tabs[:, 7 * D:] = np.asarray(inp["o_norm_w"], np.float32).reshape(-1)[None, :]
    poolw = np.asarray(inp["pool_w"], np.float32)[0]
    poolw = np.ascontiguousarray(poolw.transpose(1, 0, 2).reshape(128, 1024))
    c = np.ascontiguousarray
    return {
        "xp": xp,
        "p_own": c(np.asarray(inp["p"], np.float32)[0, b, own0:own0 + NOWN * 128]),
        "consts": make_consts(),
        "smalls": sm,
        "tabs": tabs,
        "w_kv": c(np.concatenate([w_k, w_v], 1)),
        "w_q": c(w_q),
        "w_zg": c(np.concatenate([w_z, w_gb], 1)),
        "w_ga": c(w_ga),
        "w_ba": c(np.concatenate([w_beta, w_a], 1)),
        "w_pl": c(w_pl),
        "poolw": poolw,
        "w_out": c(np.asarray(inp["w_out"], np.float32)[0]),
        "w_up": c(np.asarray(inp["w_up"], np.float32)[0]),
        "w_down": c(np.asarray(inp["w_down"], np.float32)[0]),
        "w_g": c(np.asarray(inp["ple_gate_w"], np.float32)[0]),
        "w_p": c(np.asarray(inp["ple_proj_w"], np.float32)[0]),
    }


_NC_CACHE = {}


def kernel(**inputs):
    NPRE, NOWN = 48, 16
    x = np.asarray(inputs["x"])
    B, SEQ, _ = x.shape
    key = (NPRE, NOWN)
    if key not in _NC_CACHE:
        _NC_CACHE[key] = build(NPRE, NOWN)[0]
    nc = _NC_CACHE[key]
    in_maps = []
    for c in range(8):
        in_maps.append(core_inputs(inputs, c // 4, c % 4, NPRE, NOWN, SEQ))
    res = run_bass_kernel_spmd(nc, in_maps, core_ids=list(range(8)))
    out = np.zeros((B, SEQ, D), np.float32)
    for c in range(8):
        b, j = c // 4, c % 4
        out[b, j * NOWN * 128:(j + 1) * NOWN * 128] = res.results[c]["out"]
    return out
```
